# Optimizing a Trainium2 kernel written in Bass

```python
import jax, jax.numpy as jnp
from jax import lax
import numpy as np

D_MODEL = 1024
BATCH = 8
SEQ = 2048
DEPTH = 1
DEC_BATCH = 128
DEC_SEQ = 8
PAST_LEN = 16384
PAGE_SIZE = 128

RET_HEADS = 4
RET_QK_DIM = 128
RET_V_DIM = D_MODEL // 2 // RET_HEADS
RET_WIDTH = RET_HEADS * RET_V_DIM
GLA_HEADS = 4
GLA_V_DIM = D_MODEL // 2 // GLA_HEADS
GLA_QK_DIM = GLA_V_DIM // 2
GLA_WIDTH = GLA_HEADS * GLA_V_DIM
GLA_GATE_RANK = 16
GLA_GATE_NORMALIZER = 16.0
MIX_WIDTH = RET_WIDTH + GLA_WIDTH
D_FF = -(-8 * D_MODEL // (3 * 256)) * 256
IN_SPLITS = (RET_HEADS * RET_QK_DIM, RET_HEADS * RET_QK_DIM, RET_WIDTH, RET_WIDTH,
             GLA_HEADS * GLA_QK_DIM, GLA_HEADS * GLA_QK_DIM, GLA_WIDTH, GLA_WIDTH, GLA_GATE_RANK)
IN_DIM = sum(IN_SPLITS)
RET_CHUNK = 128
GLA_CHUNK = 64
ROPE_BASE = 10000.0
EPS = 1e-6

kernel_name = "hybrid_retention_gla_adaln_step"


def rmsnorm(x, gain):
    xf = x.astype(jnp.float32)
    y = xf * lax.rsqrt(jnp.mean(xf * xf, axis=-1, keepdims=True) + EPS)
    return (y * gain.astype(jnp.float32)).astype(x.dtype)


def rotary(x, pos):
    half = x.shape[-1] // 2
    inv = ROPE_BASE ** (-jnp.arange(half, dtype=jnp.float32) / half)
    ang = pos.astype(jnp.float32)[:, None] * inv[None, :]
    cos, sin = jnp.cos(ang), jnp.sin(ang)
    x1, x2 = x[..., :half], x[..., half:]
    return jnp.concatenate([x1 * cos - x2 * sin, x1 * sin + x2 * cos], axis=-1)


def to_chunks(a, chunk):
    B, H, L, d = a.shape
    return a.reshape(B, H, L // chunk, chunk, d).transpose(2, 0, 1, 3, 4)


def from_chunks(a):
    N, B, H, C, d = a.shape
    return a.transpose(1, 2, 0, 3, 4).reshape(B, H, N * C, d)


def retention_chunked(q, k, v, s0, log_gamma, chunk):
    idx = jnp.arange(chunk, dtype=jnp.float32)
    lg = log_gamma[:, None]
    diff = idx[:, None] - idx[None, :]
    causal = diff >= 0
    decay_mat = jnp.where(causal[None], jnp.exp(lg[:, :, None] * jnp.where(causal, diff, 0.0)[None]), 0.0)
    q_dec = jnp.exp(lg * (idx + 1.0))[None, :, :, None]
    k_dec = jnp.exp(lg * (chunk - 1.0 - idx))[None, :, :, None]
    g_chunk = jnp.exp(log_gamma * chunk)[None, :, None, None]

    def step(s, inp):
        qi, ki, vi = inp
        att = jnp.einsum('bhid,bhjd->bhij', qi, ki) * decay_mat[None]
        o = jnp.einsum('bhij,bhjv->bhiv', att, vi) + jnp.einsum('bhid,bhdv->bhiv', qi * q_dec, s)
        s = s * g_chunk + jnp.einsum('bhjd,bhjv->bhdv', ki * k_dec, vi)
        return s, o

    s, o = lax.scan(step, s0, (to_chunks(q, chunk), to_chunks(k, chunk), to_chunks(v, chunk)))
    return from_chunks(o), s


def gla_chunked(q, k, v, g, s0, chunk):
    causal = jnp.tril(jnp.ones((chunk, chunk), dtype=bool))[None, None, :, :, None]

    def step(s, inp):
        qi, ki, vi, gi = inp
        b = jnp.cumsum(gi, axis=2)
        rel = jnp.where(causal, b[:, :, :, None, :] - b[:, :, None, :, :], -jnp.inf)
        att = jnp.einsum('bhid,bhjd,bhijd->bhij', qi, ki, jnp.exp(rel))
        o = jnp.einsum('bhij,bhjv->bhiv', att, vi) + jnp.einsum('bhid,bhdv->bhiv', qi * jnp.exp(b), s)
        b_last = b[:, :, -1:, :]
        s = s * jnp.exp(b_last[:, :, 0, :])[..., None] + jnp.einsum('bhjd,bhjv->bhdv', ki * jnp.exp(b_last - b), vi)
        return s, o

    s, o = lax.scan(step, s0, (to_chunks(q, chunk), to_chunks(k, chunk), to_chunks(v, chunk), to_chunks(g, chunk)))
    return from_chunks(o), s


def split_heads(a, n_heads):
    B, L, W = a.shape
    return a.reshape(B, L, n_heads, W // n_heads).transpose(0, 2, 1, 3).astype(jnp.float32)


def head_norm(o, gain):
    B, H, L, d = o.shape
    o = o.transpose(0, 2, 1, 3)
    o = o * lax.rsqrt(jnp.mean(o * o, axis=-1, keepdims=True) + EPS)
    return (o * gain.reshape(H, d).astype(jnp.float32)).reshape(B, L, H * d)


def hybrid_layer(x, c, s_ret, s_gla, pos, ret_chunk, gla_chunk,
                 w_ada, b_ada, mix_norm, w_in, w_gk_up, b_gk_up, ret_norm, gla_norm,
                 w_out, ffn_norm, w_gate_up, w_down):
    dt = x.dtype
    mod = jax.nn.silu(c) @ w_ada + b_ada
    sh1, sc1, gt1, sh2, sc2, gt2 = jnp.split(mod[:, None, :], 6, axis=-1)

    h = rmsnorm(x, mix_norm) * (1.0 + sc1) + sh1
    proj = h @ w_in
    cuts = [int(v) for v in np.cumsum(IN_SPLITS)[:-1]]
    q_r, k_r, v_r, g_r, q_g, k_g, v_g, g_g, lr = jnp.split(proj, cuts, axis=-1)

    q_r = rotary(split_heads(q_r, RET_HEADS), pos)
    k_r = rotary(split_heads(k_r, RET_HEADS), pos) * (RET_QK_DIM ** -0.5)
    log_gamma = jnp.log1p(-jnp.exp2(-5.0 - jnp.arange(RET_HEADS, dtype=jnp.float32)))
    o_r, s_ret_new = retention_chunked(q_r, k_r, split_heads(v_r, RET_HEADS),
                                       s_ret.astype(jnp.float32), log_gamma, ret_chunk)
    o_r = head_norm(o_r, ret_norm) * jax.nn.silu(g_r.astype(jnp.float32))

    gk = jax.nn.log_sigmoid((lr @ w_gk_up + b_gk_up).astype(jnp.float32)) / GLA_GATE_NORMALIZER
    q_g = split_heads(q_g, GLA_HEADS) * (GLA_QK_DIM ** -0.5)
    o_g, s_gla_new = gla_chunked(q_g, split_heads(k_g, GLA_HEADS), split_heads(v_g, GLA_HEADS),
                                 split_heads(gk, GLA_HEADS), s_gla.astype(jnp.float32), gla_chunk)
    o_g = head_norm(o_g, gla_norm) * jax.nn.silu(g_g.astype(jnp.float32))

    mix = jnp.concatenate([o_r, o_g], axis=-1).astype(dt) @ w_out
    x = x + gt1 * mix

    h2 = rmsnorm(x, ffn_norm) * (1.0 + sc2) + sh2
    a, b = jnp.split(h2 @ w_gate_up, 2, axis=-1)
    x = x + gt2 * ((jax.nn.silu(a) * b) @ w_down)
    return x, s_ret_new, s_gla_new


def setup_inputs(seed: int = 0) -> dict:
    key = jax.random.key(seed)
    ks = jax.random.split(key, 20)
    f32 = jnp.float32
    n = lambda k, shape, s: jax.random.normal(k, shape, f32) * s
    return {
        "x_prompt": n(ks[0], (BATCH, SEQ, D_MODEL), 1.0),
        "x_sample": n(ks[1], (DEC_BATCH, DEC_SEQ, D_MODEL), 1.0),
        "state_ret": n(ks[2], (DEPTH, DEC_BATCH, RET_HEADS, RET_QK_DIM, RET_V_DIM), 0.1),
        "state_gla": n(ks[3], (DEPTH, DEC_BATCH, GLA_HEADS, GLA_QK_DIM, GLA_V_DIM), 0.1),
        "c_prompt": n(ks[4], (BATCH, D_MODEL), 1.0),
        "c_sample": n(ks[5], (DEC_BATCH, D_MODEL), 1.0),
        "w_ada": n(ks[6], (DEPTH, D_MODEL, 6 * D_MODEL), 0.5 * D_MODEL ** -0.5),
        "b_ada": n(ks[7], (DEPTH, 6 * D_MODEL), 0.01),
        "mix_norm": 1.0 + n(ks[8], (DEPTH, D_MODEL), 0.02),
        "w_in": n(ks[9], (DEPTH, D_MODEL, IN_DIM), D_MODEL ** -0.5),
        "w_gk_up": n(ks[10], (DEPTH, GLA_GATE_RANK, GLA_HEADS * GLA_QK_DIM), GLA_GATE_RANK ** -0.5),
        "b_gk_up": n(ks[11], (DEPTH, GLA_HEADS * GLA_QK_DIM), 0.01),
        "ret_norm": 1.0 + n(ks[12], (DEPTH, RET_WIDTH), 0.02),
        "gla_norm": 1.0 + n(ks[13], (DEPTH, GLA_WIDTH), 0.02),
        "w_out": n(ks[14], (DEPTH, MIX_WIDTH, D_MODEL), MIX_WIDTH ** -0.5),
        "ffn_norm": 1.0 + n(ks[15], (DEPTH, D_MODEL), 0.02),
        "w_gate_up": n(ks[16], (DEPTH, D_MODEL, 2 * D_FF), D_MODEL ** -0.5),
        "w_down": n(ks[17], (DEPTH, D_FF, D_MODEL), D_FF ** -0.5),
        "final_norm": 1.0 + n(ks[18], (D_MODEL,), 0.02),
    }


def reference(x_prompt, x_sample, state_ret, state_gla, c_prompt, c_sample,
              w_ada, b_ada, mix_norm, w_in, w_gk_up, b_gk_up, ret_norm, gla_norm,
              w_out, ffn_norm, w_gate_up, w_down, final_norm):
    pos_prompt = jnp.arange(SEQ, dtype=jnp.int32)
    pos_sample = PAST_LEN + jnp.arange(DEC_SEQ, dtype=jnp.int32)
    L_p = x_prompt.shape[1]
    L_s = x_sample.shape[1]
    ret_chunk_p = min(RET_CHUNK, L_p)
    gla_chunk_p = min(GLA_CHUNK, L_p)
    zero_ret = jnp.zeros((x_prompt.shape[0], RET_HEADS, RET_QK_DIM, RET_V_DIM), jnp.float32)
    zero_gla = jnp.zeros((x_prompt.shape[0], GLA_HEADS, GLA_QK_DIM, GLA_V_DIM), jnp.float32)

    hp, hs = x_prompt, x_sample
    ret_p, gla_p, ret_s, gla_s = [], [], [], []
    for l in range(DEPTH):
        params = (w_ada[l], b_ada[l], mix_norm[l], w_in[l], w_gk_up[l], b_gk_up[l], ret_norm[l],
                  gla_norm[l], w_out[l], ffn_norm[l], w_gate_up[l], w_down[l])
        hp, sr, sg = hybrid_layer(hp, c_prompt, zero_ret, zero_gla, pos_prompt,
                                  ret_chunk_p, gla_chunk_p, *params)
        ret_p.append(sr)
        gla_p.append(sg)
        hs, sr, sg = hybrid_layer(hs, c_sample, state_ret[l], state_gla[l], pos_sample,
                                  L_s, L_s, *params)
        ret_s.append(sr)
        gla_s.append(sg)

    y_prompt = rmsnorm(hp, final_norm)
    y_sample = rmsnorm(hs, final_norm)
    state_ret_prompt = jnp.stack(ret_p)
    state_gla_prompt = jnp.stack(gla_p)
    state_ret_sample = jnp.stack(ret_s)
    state_gla_sample = jnp.stack(gla_s)
    return (y_prompt, y_sample, state_ret_prompt, state_gla_prompt, state_ret_sample, state_gla_sample)
```

```python
import numpy as np
from contextlib import ExitStack
import concourse.bass as bass
import concourse.mybir as mybir
from concourse.bass_utils import run_bass_kernel_spmd

F32 = mybir.dt.float32
BF16 = mybir.dt.bfloat16
AF = mybir.ActivationFunctionType
ALU = mybir.AluOpType
MUL, ADD = ALU.mult, ALU.add

ENGS = ("pe", "act", "dve", "pool", "sp")
import os
KSTAGE = int(os.environ.get("KSTAGE", "99"))


class _Stop(Exception):
    pass


def _stage(n):
    if KSTAGE == n:
        raise _Stop()


class Buf:
    __slots__ = ("name", "w", "r", "rd", "sem", "cnt", "extra")

    def __init__(self, name):
        self.name = name
        self.w = None
        self.r = {}
        self.rd = []
        self.sem = None
        self.cnt = 0
        self.extra = []


class Op:
    __slots__ = ("eng", "fn", "deps", "sig", "dma", "dbuf", "val", "sidx")

    def __init__(self, eng, fn, dma, dbuf):
        self.eng = eng
        self.fn = fn
        self.deps = []
        self.sig = False
        self.dma = dma
        self.dbuf = dbuf
        self.val = 0
        self.sidx = 0


class Prog:
    def __init__(self, nc):
        self.nc = nc
        self.ops = {e: [] for e in ENGS}
        self.all = []
        self.nbuf = 0

    def buf(self, name=None):
        self.nbuf += 1
        return Buf(name or f"b{self.nbuf}")

    def bufs(self, n, name="b"):
        return [self.buf(f"{name}{i}") for i in range(n)]

    def alias(self, dst, src):
        ops = []
        for s in src:
            if s.w is not None:
                ops.append(s.w)
            ops.extend(s.r.values())
            ops.extend(s.rd)
            ops.extend(s.extra)
        for d in dst:
            d.extra.extend(ops)

    def op(self, eng, fn, reads=(), writes=(), dma=False, dbuf=None, extra=()):
        o = Op(eng, fn, dma, dbuf)
        if dma:
            qt = "sw" if eng == "pool" else "hw"
            if dbuf.sem is None:
                dbuf.sem = {}
            ent = dbuf.sem.setdefault(qt, [None, 0])
            ent[1] += 16
            o.val = ent[1]
            o.sidx = qt
            o.sig = True
        deps = {}

        def add(d, raw):
            if d is None:
                return
            if (not d.dma) and (not dma) and d.eng == eng:
                if eng == "pe" or not raw:
                    return
            deps[id(d)] = d

        for b in reads:
            add(b.w, True)
        for b in writes:
            add(b.w, False)
            for r in b.r.values():
                add(r, False)
            for r in b.rd:
                add(r, False)
            for r in b.extra:
                add(r, False)
        for d in extra:
            deps[id(d)] = d
        o.deps = list(deps.values())
        for d in o.deps:
            d.sig = True
        for b in writes:
            b.w = o
            b.r = {}
            b.rd = []
            b.extra = []
        for b in reads:
            if dma:
                b.rd.append(o)
            else:
                b.r[eng] = o
        self.ops[eng].append(o)
        self.all.append(o)
        return o

    def emit(self, es):
        nc = self.nc
        esem = {e: es.enter_context(nc.semaphore(f"s_{e}")) for e in ENGS}
        nsem = len(ENGS)
        for o in self.all:
            if o.dma and o.dbuf.sem[o.sidx][0] is None:
                o.dbuf.sem[o.sidx][0] = es.enter_context(nc.semaphore(f"d{o.sidx}_{o.dbuf.name}"))
                nsem += 1
        self.nsem = nsem
        for e in ENGS:
            c = 0
            for o in self.ops[e]:
                if o.sig and not o.dma:
                    c += 1
                    o.sidx = c
        block = es.enter_context(nc.Block())

        def run(e, handle):
            seen = {}
            for o in self.ops[e]:
                ws = {}
                for d in o.deps:
                    if d.dma:
                        s, v = d.dbuf.sem[d.sidx][0], d.val
                    else:
                        s, v = esem[d.eng], d.sidx
                    k = id(s)
                    if seen.get(k, 0) >= v:
                        continue
                    if k not in ws or ws[k][1] < v:
                        ws[k] = (s, v)
                for k, (s, v) in ws.items():
                    handle.wait_ge(s, v)
                    seen[k] = v
                ins = o.fn(handle)
                if o.dma:
                    ins.then_inc(o.dbuf.sem[o.sidx][0], 16)
                elif o.sig:
                    ins.then_inc(esem[e], 1)

        @block.tensor
        def _(h):
            run("pe", h)

        @block.scalar
        def _(h):
            run("act", h)

        @block.vector
        def _(h):
            run("dve", h)

        @block.gpsimd
        def _(h):
            run("pool", h)

        @block.sync
        def _(h):
            run("sp", h)


D = 1024
NFC = 8
SEQ = 2048
NSQ = 16
DSEQ = 8
PAST = 16384
DFF = 2816
NKF = 22
INDIM = 3600
EPS = 1e-6
NCORES = int(os.environ.get('KCORES', '8'))
BLOCKS = [[("p", 0), ("p", 1), ("p", 2), ("p", 3), ("s", 0)],
          [("p", 4), ("p", 5), ("p", 6), ("p", 7)],
          [("p", 8), ("p", 9), ("p", 10), ("p", 11)],
          [("p", 12), ("p", 13), ("p", 14), ("p", 15)]]
TBMAX = 640
NTOK = SEQ + NSQ * DSEQ
GAM = [1.0 - 2.0 ** (-5 - h) for h in range(4)]

C_E1R, C_E2R, C_EHR, C_E1S, C_E2S, C_EHS = 0, 512, 1024, 1536, 2048, 2560
C_M01, C_MSM, C_MSEL, C_ID, C_SMP, C_SMS = 3072, 3200, 3328, 3344, 3472, 3600
NCR = 3728
V_MIXN, V_FFNN, V_FINN, V_RETN, V_GLAN, V_BGK, V_BADA, NV = 0, 8, 16, 24, 28, 32, 34, 82


def host_consts():
    half = 64
    inv = 10000.0 ** (-np.arange(half, dtype=np.float64) / half)
    pos_p = np.arange(SEQ, dtype=np.float64)
    pos_s = np.tile(PAST + np.arange(DSEQ, dtype=np.float64), NSQ)
    pos = np.concatenate([pos_p[:512], pos_s, pos_p[512:]])
    ang = pos[None, :] * np.concatenate([inv, inv])[:, None]
    cosT = np.cos(ang)
    sw = np.sin(ang)
    sw[64:] *= -1.0
    cossw = np.stack([cosT, sw], axis=1).astype(np.float32)
    cst = np.zeros((128, NCR), np.float64)
    t = np.arange(128, dtype=np.float64)
    ts = t % 8
    for h in range(4):
        lg = np.log1p(-2.0 ** (-5 - h))
        cst[:, C_E1R + 128 * h:C_E1R + 128 * h + 128] = np.exp(lg * (t + 1))[None]
        cst[:, C_E2R + 128 * h:C_E2R + 128 * h + 128] = (np.exp(-lg * (t + 1)) * 128 ** -0.5)[None]
        cst[:, C_EHR + 128 * h:C_EHR + 128 * h + 128] = (np.exp(lg * (127 - t)) * 128 ** -0.5)[None]
        cst[:, C_E1S + 128 * h:C_E1S + 128 * h + 128] = np.exp(lg * (ts + 1))[None]
        cst[:, C_E2S + 128 * h:C_E2S + 128 * h + 128] = (np.exp(-lg * (ts + 1)) * 128 ** -0.5)[None]
        cst[:, C_EHS + 128 * h:C_EHS + 128 * h + 128] = (np.exp(lg * (7 - ts)) * 128 ** -0.5)[None]
    j = np.arange(128)[:, None]
    i = np.arange(128)[None, :]
    cst[:, C_M01:C_M01 + 128] = (i >= j)
    cst[:, C_MSM:C_MSM + 128] = (i >= j) & ((i // 8) == (j // 8))
    cst[:, C_MSEL:C_MSEL + 16] = ((j // 8) == np.arange(16)[None, :])
    cst[:, C_ID:C_ID + 128] = (i == j)
    cst[:, C_SMP:C_SMP + 128] = (i != 0)
    cst[:, C_SMS:C_SMS + 128] = ((i % 8) != 0)
    return cst.astype(np.float32), cossw


def build():
    nc = bass.Bass("TRN2", target_bir_lowering=False)

    def din(name, shape):
        return nc.dram_tensor(name, list(shape), F32, kind="ExternalInput").ap()

    def dout(name, shape):
        return nc.dram_tensor(name, list(shape), F32, kind="ExternalOutput").ap()

    xp_d = din("xp", [SEQ, D])
    xs_d = din("xs", [128, D])
    c_d = din("c17", [17, D])
    sret_d = din("sret", [NSQ, 512, 128])
    sgla_d = din("sgla", [NSQ, 256, 128])
    wada_d = din("w_ada", [D, 6 * D])
    win_d = din("w_in", [D, INDIM])
    wgk_d = din("w_gk", [16, 256])
    wout_d = din("w_out", [D, D])
    wgu_d = din("w_gu", [D, 2 * DFF])
    wdn_d = din("w_dn", [DFF, D])
    vec_d = din("vecs", [NV, 128])
    cst_d = din("cst", [128, NCR])
    cs_d = din("cossw", [128, 2, NTOK])
    yp_d = dout("yp", [SEQ, D])
    ys_d = dout("ys", [128, D])
    srp_d = dout("srp", [512, 128])
    sgp_d = dout("sgp", [256, 128])
    srs_d = dout("srs", [NSQ, 512, 128])
    sgs_d = dout("sgs", [NSQ, 256, 128])

    es = ExitStack()
    P = Prog(nc)

    def sb(name, shape, dt=F32):
        return es.enter_context(nc.sbuf_tensor("sb_" + name, list(shape), dt))

    pb = [es.enter_context(nc.psum_tensor(f"pb{i}", [128, 512], F32)) for i in range(7)]
    pb7 = es.enter_context(nc.psum_tensor("pb7", [128, 1024], BF16))
    Bpb = P.bufs(8, "pb")

    def mm(out, lhsT, rhs, start, stop, reads, writes, sgc=False):
        return P.op("pe", lambda h: h.matmul(out, lhsT=lhsT, rhs=rhs, start=start, stop=stop, skip_group_check=sgc), reads, writes)

    def tr(out, in_, ident, reads, writes):
        return P.op("pe", lambda h: h.transpose(out=out, in_=in_, identity=ident), reads, writes)

    def act(out, in_, func, reads, writes, **kw):
        return P.op("act", lambda h: h.activation(out=out, in_=in_, func=func, **kw), reads, writes)

    def tt(eng, out, in0, in1, op, reads, writes):
        return P.op(eng, lambda h: h.tensor_tensor(out=out, in0=in0, in1=in1, op=op), reads, writes)

    def stt(out, in0, scalar, in1, op0, op1, reads, writes):
        return P.op("dve", lambda h: h.scalar_tensor_tensor(out=out, in0=in0, scalar=scalar, in1=in1, op0=op0, op1=op1), reads, writes)

    def tsc(eng, out, in0, s1, s2, op0, op1, reads, writes):
        if s2 is None and eng == "pool":
            s2, op1 = 0.0, ADD
        if s2 is None:
            return P.op(eng, lambda h: h.tensor_scalar(out=out, in0=in0, scalar1=s1, scalar2=None, op0=op0), reads, writes)
        return P.op(eng, lambda h: h.tensor_scalar(out=out, in0=in0, scalar1=s1, scalar2=s2, op0=op0, op1=op1), reads, writes)

    def cp(eng, out, in_, reads, writes):
        if eng == "act":
            return act(out, in_, AF.Copy, reads, writes)
        return P.op(eng, lambda h: h.tensor_copy(out=out, in_=in_), reads, writes)

    all_dmas = []

    def dma(q, out, in_, reads, writes, dbuf):
        o = P.op(q, lambda h: h.dma_start(out=out, in_=in_), reads, writes, dma=True, dbuf=dbuf)
        all_dmas.append(o)
        return o

    def mset(eng, ap, val, writes):
        return P.op(eng, lambda h: h.memset(ap, val), (), writes)

    out_dmas = []

    cst = sb("cst", [128, NCR])
    Bcst = P.buf("cst")
    identb = sb("identb", [128, 128], BF16)
    onesb = sb("onesb", [128, 128], BF16)
    onesf = sb("onesf", [128, 128])
    Bid = P.buf("idb")
    Bones = P.buf("ones")
    vin = sb("vin", [NV, 128])
    vc = sb("vc", [128, NV])
    nbgk = sb("nbgk", [128, 2])
    Bvin, Bvc = P.buf("vin"), P.buf("vc")
    cT = sb("cT", [128, 8, 17], BF16)
    BcT = P.buf("cT")
    modT = sb("modT", [128, 48, 17])
    g1 = sb("g1", [128, 8, 17])
    g2 = sb("g2", [128, 8, 17])
    Bmod = P.buf("mod")
    wgk = sb("wgk", [16, 256], BF16)
    Bwgk = P.buf("wgk")
    smask = sb("smask", [128, TBMAX])
    Bsmask = P.buf("smask")

    ident = cst[:, C_ID:C_ID + 128]

    NSLOT = 5
    wslots = [sb(f"wslot{i}", [128, 2048], BF16) for i in range(NSLOT)]
    Bws = P.bufs(NSLOT, "ws")

    xT = sb("xT", [128, 8, TBMAX])
    BxT = [[P.buf(f"xT{fc}_{i}") for i in range(5)] for fc in range(8)]

    def bx(fc, c0, n):
        return BxT[fc][c0 // 128:(c0 + n) // 128]
    hT = sb("hT", [128, 8, TBMAX], BF16)
    Bh = [[P.buf(f"h{fc}_{i}") for i in range(5)] for fc in range(8)]
    cossw = sb("cossw", [128, 2, TBMAX])
    Bcs = P.buf("cossw")
    rsb = sb("rsb", [128, TBMAX])
    Brsbs = P.bufs(5, "rsb")

    def brs(c0, n):
        return Brsbs[c0 // 128:(c0 + n) // 128]
    A_QT, A_KT, A_KH = 0, 4 * TBMAX, 8 * TBMAX
    A_V = 12 * TBMAX
    A_SG = A_V + 5 * 1024
    A_QTG = A_SG + 8 * TBMAX
    A_KTAB = A_QTG + 2 * TBMAX
    A_KHG = A_KTAB + 4 * TBMAX
    A_END = A_KHG + 2 * TBMAX
    assert A_END >= NKF * TBMAX
    arena = sb("arena", [128, A_END], BF16)

    def aview(off, n, w):
        return arena[:, off:off + n * w].rearrange("p (a b) -> p a b", b=w)

    qt_r, kt_r, kh_r = aview(A_QT, 4, TBMAX), aview(A_KT, 4, TBMAX), aview(A_KH, 4, TBMAX)
    v_tok = aview(A_V, 5, 1024)
    sg = aview(A_SG, 8, TBMAX)
    qt_g, ktAB, kh_g = aview(A_QTG, 2, TBMAX), aview(A_KTAB, 4, TBMAX), aview(A_KHG, 2, TBMAX)
    hid = aview(0, NKF, TBMAX)
    Bqt, Bkt, Bkh = P.bufs(4, "qt"), P.bufs(4, "kt"), P.bufs(4, "kh")
    Bv = P.bufs(5, "v")
    Bsg = P.bufs(8, "sg")
    Bqtg, Bktab, Bkhg = P.bufs(2, "qtg"), P.bufs(2, "ktab"), P.bufs(2, "khg")
    Bhid = P.bufs(NKF, "hid")
    mixer_bufs = Bqt + Bkt + Bkh + Bv + Bsg + Bqtg + Bktab + Bkhg

    lrT = sb("lrT", [16, TBMAX], BF16)
    Blr = P.buf("lr")
    spb = sb("spb", [128, 2, TBMAX])
    bsum = sb("bsum", [128, 2, TBMAX])
    E1, E2 = bsum, spb
    Bspb, Bbsum = P.bufs(2, "spb"), P.bufs(2, "bsum")
    BE1, BE2 = Bbsum, Bspb
    NTMP = 6
    tmps = [sb(f"tmp{i}", [128, 512]) for i in range(NTMP)]
    Btmp = P.bufs(NTMP, "tmp")
    tmp_i = [0]

    def tmp():
        k = tmp_i[0] % NTMP
        tmp_i[0] += 1
        return tmps[k], Btmp[k]

    xin = [sb(f"xin{i}", [128, D]) for i in range(2)]
    Bxin = P.bufs(2, "xin")
    cin, Bcin = xin[1], Bxin[1]
    ssq = sb("ssq", [128, 2])
    diag = [sb(f"diag{i}", [128, 128]) for i in range(2)]
    Bssq, Bdiag = P.bufs(2, "ssq"), P.bufs(2, "diag")

    attm = [sb(f"attm{i}", [128, 1024], BF16) for i in range(2)]
    Battm = P.bufs(2, "attm")
    osq = [sb(f"osq{i}", [128, 1024], BF16) for i in range(2)]
    Bosq = [P.bufs(2, f"osq{i}") for i in range(2)]
    rinv = sb("rinv", [128, 1024])
    Brinv = P.bufs(2, "rinv")
    osb = [sb(f"osb{i}", [128, 1024]) for i in range(2)]
    Bosb = [P.bufs(2, f"osb{i}") for i in range(2)]
    sqj, Bsqj = osq[0], Bosq[0][0]
    xsq = [osq[0][:, 0:512], osq[1][:, 0:512]]
    Bxsq = [Bosq[0][0], Bosq[1][0]]
    yT = [osb[0][:].rearrange("p (a b) -> p a b", b=128), osb[1][:].rearrange("p (a b) -> p a b", b=128)]
    ByT = [Bosb[0], Bosb[1]]
    NKHT = 2
    khT = [sb(f"khT{i}", [128, 768], BF16) for i in range(NKHT)]
    BkhT = P.bufs(NKHT, "khT")
    khT_i = [0]

    NSS = 4

    class State:
        pass

    states = []
    for k in range(1 + NSS):
        st = State()
        st.rf = sb(f"Srf{k}", [128, 4, 128])
        st.gf = sb(f"Sgf{k}", [128, 2, 128])
        st.rb = sb(f"Srb{k}", [128, 4, 128], BF16)
        st.gb = sb(f"Sgb{k}", [128, 4, 128], BF16)
        st.Brf, st.Bgf, st.Brb, st.Bgb = P.bufs(4, f"Srf{k}_"), P.bufs(4, f"Sgf{k}_"), P.buf(f"Srb{k}"), P.buf(f"Sgb{k}")
        states.append(st)

    jobs = []
    NJB = 15 + 4 + 22 + 16
    wscr = nc.dram_tensor("wscr", [NJB, 128, 2048], BF16, kind="Internal").ap()
    Bscr = P.bufs(NJB, "wscr")

    def job_cols(w, c0, ncols, cidx=None, fresh=True, wc=False):
        src = w[:, c0:c0 + ncols].rearrange("(kc k) c -> k kc c", k=128) if fresh else None
        return (lambda s: s[:, 0:8 * ncols].rearrange("p (k c) -> p k c", c=ncols), src, cidx, 8 * ncols, wc)

    def job_down(m, half, cidx=None, fresh=True, wc=False):
        src = wdn_d[half * 1408:(half + 1) * 1408, 128 * m:128 * m + 128].rearrange("(kc k) c -> k kc c", k=128) if fresh else None
        return (lambda s: s[:, 0:11 * 128].rearrange("p (k c) -> p k c", c=128), src, cidx, 11 * 128, wc)

    for t_ in range(8):
        jobs.append(job_cols(wada_d, 256 * t_, 256))
    WIN_ORDER = [("lr", 3584, 16), ("v", 1024, 256), ("v", 1280, 256), ("v", 2560, 256), ("v", 2816, 256),
                 ("qg", 2048, 256), ("kg", 2304, 256), ("qr", 0, 256), ("qr", 256, 256), ("kr", 512, 256), ("kr", 768, 256),
                 ("gr", 1536, 256), ("gr", 1792, 256), ("gg", 3072, 256), ("gg", 3328, 256)]
    for b_ in range(len(BLOCKS)):
        fr = b_ == 0
        ffr = b_ <= 1
        fwc = b_ == 1
        ci = 0
        for k_, (_, c0, n_) in enumerate(WIN_ORDER):
            jobs.append(job_cols(win_d, c0, n_, ci, fr, fr))
            ci += 1
            if b_ == 0:
                jobs.append(job_cols(wada_d, 256 * (8 + k_), 256))
        if b_ == 0:
            jobs.append(job_cols(wada_d, 256 * 23, 256))
        for t_ in range(4):
            jobs.append(job_cols(wout_d, 256 * t_, 256, ci, fr, fr))
            ci += 1
        for j_ in range(11):
            jobs.append(job_cols(wgu_d, 256 * j_, 256, ci, ffr, fwc))
            jobs.append(job_cols(wgu_d, DFF + 256 * j_, 256, ci + 1, ffr, fwc))
            ci += 2
        for m_ in range(8):
            jobs.append(job_down(m_, 0, ci, ffr, fwc))
            jobs.append(job_down(m_, 1, ci + 1, ffr, fwc))
            ci += 2
        assert ci == NJB
    wstate = {"issued": 0, "next": 0}

    def wget():
        n = wstate["next"]
        wstate["next"] += 1
        while wstate["issued"] < min(n + NSLOT - 1, len(jobs)):
            k = wstate["issued"]
            vf, src, cidx, L, wc = jobs[k]
            s = k % NSLOT
            if src is not None:
                dma("pool", vf(wslots[s]), src, (), [Bws[s]], Bws[s])
                if wc:
                    dma("sp", wscr[cidx][:, 0:L], wslots[s][:, 0:L], [Bws[s]], [Bscr[cidx]], Bws[s])
            else:
                dma("sp", wslots[s][:, 0:L], wscr[cidx][:, 0:L], [Bscr[cidx]], [Bws[s]], Bws[s])
            wstate["issued"] += 1
        vf = jobs[n][0]
        return vf(wslots[n % NSLOT]), Bws[n % NSLOT]

    dma("sp", cst[:], cst_d, (), [Bcst], Bcst)
    dma("sp", vin[:], vec_d, (), [Bvin], Bvin)
    dma("sp", cin[0:17, :], c_d, (), [Bcin], Bcin)
    dma("pool", wgk[:], wgk_d, (), [Bwgk], Bwgk)
    cp("dve", identb[:], ident, [Bcst], [Bid])
    mset("pool", onesb[:], 1.0, [Bones])
    mset("pool", onesf[:], 1.0, [Bones])
    mset("pool", arena[:, A_KTAB:A_KTAB + 4 * TBMAX], 0.0, Bktab)
    for st in states:
        mset("pool", st.gb[:], 0.0, [st.Bgb])
    mset("pool", states[0].rf[:], 0.0, states[0].Brf)
    mset("pool", states[0].gf[:], 0.0, states[0].Bgf)
    cp("pool", smask[:, 0:512].rearrange("p (c t) -> p c t", t=128),
       cst[:, C_SMP:C_SMP + 128].unsqueeze(1).broadcast_to([128, 4, 128]), [Bcst], [Bsmask])
    cp("pool", smask[:, 512:640], cst[:, C_SMS:C_SMS + 128], [Bcst], [Bsmask])
    tr(pb[2][:, 0:NV], vin[0:NV, :], cst[0:NV, C_ID:C_ID + NV], [Bvin, Bcst], [Bpb[2]])
    cp("dve", vc[:], pb[2][:, 0:NV], [Bpb[2]], [Bvc])
    tsc("dve", nbgk[:], vc[:, V_BGK:V_BGK + 2], -1.0, None, MUL, None, [Bvc], [Bvc])
    for kc in range(8):
        tr(pb[3][:, kc * 17:(kc + 1) * 17], cin[0:17, kc * 128:(kc + 1) * 128], cst[0:17, C_ID:C_ID + 17], [Bcin, Bcst], [Bpb[3]])
    act(cT[:], pb[3][:, 0:136].rearrange("p (k s) -> p k s", s=17), AF.Silu, [Bpb[3]], [BcT])
    xloaded = set()

    def x_dma(bi, i):
        if (bi, i) in xloaded:
            return
        xloaded.add((bi, i))
        kind, idx = BLOCKS[bi][i]
        sl = i % 2
        src = xp_d[128 * idx:128 * idx + 128, :] if kind == "p" else xs_d
        dma("sp", xin[sl][:], src, (), [Bxin[sl]], Bxin[sl])

    def p1a_begin(bi):
        TB = 128 * len(BLOCKS[bi])
        col0 = 0 if bi == 0 else 640 + 512 * (bi - 1)
        dma("sp", cossw[:, :, 0:TB], cs_d[:, :, col0:col0 + TB], (), [Bcs], Bcs)

    def p1a_chunk(bi, i):
        c0 = 128 * i
        sl = i % 2
        x_dma(bi, i)
        act(sqj[:], xin[sl][:], AF.Square, [Bxin[sl]], [Bsqj, Bosq[0][1], Bssq[sl]], accum_out=ssq[:, sl:sl + 1])
        tsc("dve", diag[sl][:], ident, ssq[:, sl:sl + 1], None, MUL, None, [Bcst, Bssq[sl]], [Bdiag[sl]])
        for fc in range(8):
            bk = fc // 4
            tr(pb[bk][:, (fc % 4) * 128:(fc % 4) * 128 + 128], xin[sl][:, fc * 128:fc * 128 + 128], ident, [Bxin[sl], Bcst], [Bpb[bk]])
        mm(pb[6][:, 0:128], onesf[:], diag[sl][:], True, True, [Bones, Bdiag[sl]], [Bpb[6]])
        cp("act", xT[:, 0:4, c0:c0 + 128], pb[0][:].rearrange("p (a b) -> p a b", b=128), [Bpb[0]], [BxT[fc][i] for fc in range(4)])
        cp("dve", xT[:, 4:8, c0:c0 + 128], pb[1][:].rearrange("p (a b) -> p a b", b=128), [Bpb[1]], [BxT[fc][i] for fc in range(4, 8)])
        act(rsb[:, c0:c0 + 128], pb[6][:, 0:128], AF.Ln, [Bpb[6]], brs(c0, 128), scale=1.0 / D, bias=EPS)
        act(rsb[:, c0:c0 + 128], rsb[:, c0:c0 + 128], AF.Exp, brs(c0, 128), brs(c0, 128), scale=-0.5)

    p1a_begin(0)
    x_dma(0, 0)
    x_dma(0, 1)
    for i_ in range(len(BLOCKS[0])):
        p1a_chunk(0, i_)
        if i_ + 2 < len(BLOCKS[0]):
            x_dma(0, i_ + 2)
    def ada_tile(t_):
        wt, Bw = wget()
        bank = 4 + (t_ % 2)
        for o2 in range(2):
            for kc in range(8):
                mm(pb[bank][:, o2 * 17:(o2 + 1) * 17], wt[:, kc, o2 * 128:(o2 + 1) * 128], cT[:, kc, :], kc == 0, kc == 7, [Bw, BcT], [Bpb[bank]])
        for o2 in range(2):
            oc = 2 * t_ + o2
            act(modT[:, oc, :], pb[bank][:, o2 * 17:(o2 + 1) * 17], AF.Identity, [Bpb[bank], Bvc], [Bmod], bias=vc[:, V_BADA + oc:V_BADA + oc + 1])

    for t_ in range(8):
        ada_tile(t_)
    for fc in range(8):
        tsc("dve", g1[:, fc, :], modT[:, 8 + fc, :], 1.0, vc[:, V_MIXN + fc:V_MIXN + fc + 1], ADD, MUL, [Bmod, Bvc], [Bmod])

    def ada_finish():
        for fc in range(8):
            tsc("dve", g2[:, fc, :], modT[:, 32 + fc, :], 1.0, vc[:, V_FFNN + fc:V_FFNN + fc + 1], ADD, MUL, [Bmod, Bvc], [Bmod])

    def bc_seq(ap2d):
        return ap2d[:, 1:17].unsqueeze(2).broadcast_to([128, NSQ, DSEQ])

    def v3(ap):
        return ap.rearrange("p (s t) -> p s t", t=DSEQ)

    def run_block(bi):
        chs = BLOCKS[bi]
        nch = len(chs)
        has_s = chs[-1][0] == "s"
        npr = nch - (1 if has_s else 0)
        NP = 128 * npr
        TB = 128 * nch
        sbs = [(0, NP)] + ([(NP, 128)] if has_s else [])
        col0 = 0 if bi == 0 else 640 + 512 * (bi - 1)

        def hb(kc, c0, n):
            return Bh[kc][c0 // 128:(c0 + n) // 128]


        def modulate(gX, shoff):
            for fc in range(8):
                tm, Btm = tmp()
                stt(tm[:, 0:NP], xT[:, fc, 0:NP], gX[:, fc, 0:1], rsb[:, 0:NP], MUL, MUL, bx(fc, 0, NP) + [Bmod] + brs(0, NP), [Btm])
                act(hT[:, fc, 0:NP], tm[:, 0:NP], AF.Identity, [Btm, Bmod], hb(fc, 0, NP), bias=modT[:, shoff + fc, 0:1])
                if has_s:
                    t2, Bt2 = tmp()
                    tt("dve", t2[:, 0:128], xT[:, fc, NP:TB], rsb[:, NP:TB], MUL, bx(fc, NP, 128) + brs(NP, 128), [Bt2])
                    tt("pool", v3(t2[:, 0:128]), v3(t2[:, 0:128]), bc_seq(gX[:, fc, :]), MUL, [Bt2, Bmod], [Bt2])
                    tt("pool", v3(hT[:, fc, NP:TB]), v3(t2[:, 0:128]), bc_seq(modT[:, shoff + fc, :]), ADD, [Bt2, Bmod], hb(fc, NP, 128))

        modulate(g1, 0)
        _stage(1)

        if bi > 0:
            P.alias(mixer_bufs, Bhid)
        pbanks = [0, 1, 2, 3, 4, 5]
        pbi = [0]

        def nbank():
            k = pbanks[pbi[0] % len(pbanks)]
            pbi[0] += 1
            return k

        def proj(wt, Bw, ch, evac):
            for (c0, n) in sbs:
                bk = nbank()
                for kc in range(8):
                    mm(pb[bk][:, 0:n], wt[:, kc, ch * 128:ch * 128 + 128], hT[:, kc, c0:c0 + n], kc == 0, kc == 7, [Bw] + hb(kc, c0, n), [Bpb[bk]])
                evac(bk, c0, n)

        def etab(base, hh, c0, n):
            if c0 >= NP:
                return cst[:, base + 1536 + 128 * hh:base + 1536 + 128 * hh + 128]
            return cst[:, base + 128 * hh:base + 128 * hh + 128].unsqueeze(1).broadcast_to([128, n // 128, 128])

        def ch3(ap, c0, n):
            if c0 >= NP:
                return ap
            return ap.rearrange("p (c t) -> p c t", t=128)

        def rotary(bk, c0, n, add_eng="pool"):
            t1, B1 = tmp()
            t2, B2 = tmp()
            ps = pb[bk]
            tt("dve", t1[:, 0:n], ps[:, 0:n], cossw[:, 0, c0:c0 + n], MUL, [Bpb[bk], Bcs], [B1])
            tt("dve", t2[64:128, 0:n], ps[0:64, 0:n], cossw[0:64, 1, c0:c0 + n], MUL, [Bpb[bk], Bcs], [B2])
            tt("dve", t2[0:64, 0:n], ps[64:128, 0:n], cossw[64:128, 1, c0:c0 + n], MUL, [Bpb[bk], Bcs], [B2])
            tt(add_eng, t1[:, 0:n], t1[:, 0:n], t2[:, 0:n], ADD, [B1, B2], [B1])
            return t1, B1

        def ev_qr(hh):
            def f(bk, c0, n):
                r, Br = rotary(bk, c0, n)
                tt("pool", ch3(qt_r[:, hh, c0:c0 + n], c0, n), ch3(r[:, 0:n], c0, n), etab(C_E1R, hh, c0, n), MUL, [Br, Bcst], [Bqt[hh]])
            return f

        def ev_kr(hh):
            def f(bk, c0, n):
                r, Br = rotary(bk, c0, n, "dve")
                tt("pool", ch3(kt_r[:, hh, c0:c0 + n], c0, n), ch3(r[:, 0:n], c0, n), etab(C_E2R, hh, c0, n), MUL, [Br, Bcst], [Bkt[hh]])
                tt("pool", ch3(kh_r[:, hh, c0:c0 + n], c0, n), ch3(r[:, 0:n], c0, n), etab(C_EHR, hh, c0, n), MUL, [Br, Bcst], [Bkh[hh]])
            return f

        def ev_gate(hh):
            gcol = (V_RETN + hh) if hh < 4 else (V_GLAN + hh - 4)

            def f(bk, c0, n):
                tm, Btm = tmp()
                act(tm[:, 0:n], pb[bk][:, 0:n], AF.Silu, [Bpb[bk]], [Btm])
                act(sg[:, hh, c0:c0 + n], tm[:, 0:n], AF.Copy, [Btm, Bvc], [Bsg[hh]], scale=vc[:, gcol:gcol + 1])
            return f

        def ev_qg(p):
            def f(bk, c0, n):
                stt(qt_g[:, p, c0:c0 + n], pb[bk][:, 0:n], 0.125, E1[:, p, c0:c0 + n], MUL, MUL, [Bpb[bk], BE1[p]], [Bqtg[p]])
            return f

        def ev_kg(p):
            def f(bk, c0, n):
                tm, Btm = tmp()
                tt("dve", tm[:, 0:n], pb[bk][:, 0:n], E2[:, p, c0:c0 + n], MUL, [Bpb[bk], BE2[p]], [Btm])
                cp("act", ktAB[0:64, 2 * p, c0:c0 + n], tm[0:64, 0:n], [Btm], [Bktab[p]])
                cp("act", ktAB[64:128, 2 * p + 1, c0:c0 + n], tm[64:128, 0:n], [Btm], [Bktab[p]])
                if c0 >= NP:
                    e1l = v3(E1[:, p, c0:c0 + n])[:, :, DSEQ - 1:DSEQ].broadcast_to([128, NSQ, DSEQ])
                    tt("pool", v3(kh_g[:, p, c0:c0 + n]), v3(tm[:, 0:n]), e1l, MUL, [Btm, BE1[p]], [Bkhg[p]])
                else:
                    e1l = E1[:, p, c0:c0 + n].rearrange("p (c t) -> p c t", t=128)[:, :, 127:128].broadcast_to([128, n // 128, 128])
                    tt("pool", ch3(kh_g[:, p, c0:c0 + n], c0, n), ch3(tm[:, 0:n], c0, n), e1l, MUL, [Btm, BE1[p]], [Bkhg[p]])
            return f

        vt_i = [0]
        for wk_, (kind, wc0, wn) in enumerate(WIN_ORDER):
            if bi == 0 and wk_ > 0:
                ada_tile(8 + wk_ - 1)
            wt, Bw = wget()
            if kind == "lr":
                for (c0, n) in sbs:
                    bk = nbank()
                    for kc in range(8):
                        mm(pb[bk][0:16, 0:n], wt[:, kc, 0:16], hT[:, kc, c0:c0 + n], kc == 0, kc == 7, [Bw] + hb(kc, c0, n), [Bpb[bk]])
                    cp("act", lrT[0:16, c0:c0 + n], pb[bk][0:16, 0:n], [Bpb[bk]], [Blr])
                for p in range(2):
                    for (c0, n) in sbs:
                        bk = nbank()
                        mm(pb[bk][:, 0:n], wgk[0:16, 128 * p:128 * p + 128], lrT[0:16, c0:c0 + n], True, True, [Bwgk, Blr], [Bpb[bk]])
                        tm, Btm = tmp()
                        act(tm[:, 0:n], pb[bk][:, 0:n], AF.Exp, [Bpb[bk], Bvc], [Btm], scale=-1.0, bias=nbgk[:, p:p + 1])
                        act(spb[:, p, c0:c0 + n], tm[:, 0:n], AF.Ln, [Btm], [Bspb[p]], bias=1.0)
                    P.op("dve", lambda h, p=p: h.tensor_tensor_scan(out=bsum[:, p, 0:TB], data0=smask[:, 0:TB], data1=spb[:, p, 0:TB], initial=0.0, op0=MUL, op1=ADD),
                         [Bsmask, Bspb[p]], [Bbsum[p]])
                    act(E2[:, p, 0:TB], bsum[:, p, 0:TB], AF.Exp, [Bbsum[p]], [BE2[p]], scale=1.0 / 16.0)
                    act(E1[:, p, 0:TB], bsum[:, p, 0:TB], AF.Exp, [Bbsum[p]], [BE1[p]], scale=-1.0 / 16.0)
            elif kind == "v":
                vt = vt_i[0]
                vt_i[0] += 1
                for i in range(nch):
                    bk = nbank()
                    for kc in range(8):
                        mm(pb[bk][:, 0:256], hT[:, kc, 128 * i:128 * i + 128], wt[:, kc, 0:256], kc == 0, kc == 7, [Bw, Bh[kc][i]], [Bpb[bk]])
                    cp("act", v_tok[:, i, 256 * vt:256 * vt + 256], pb[bk][:, 0:256], [Bpb[bk]], [Bv[i]])
            else:
                for ch in range(2):
                    idx = (wc0 % 512) // 128 + ch if kind in ("qr", "kr", "gr", "gg") else ch
                    if kind == "qr":
                        proj(wt, Bw, ch, ev_qr(idx))
                    elif kind == "kr":
                        proj(wt, Bw, ch, ev_kr(idx))
                    elif kind == "gr":
                        proj(wt, Bw, ch, ev_gate(idx))
                    elif kind == "gg":
                        proj(wt, Bw, ch, ev_gate(4 + idx))
                    elif kind == "qg":
                        proj(wt, Bw, ch, ev_qg(ch))
                    elif kind == "kg":
                        proj(wt, Bw, ch, ev_kg(ch))

        if bi == 0:
            ada_tile(22)
            ada_tile(23)
            ada_finish()
        _stage(2)
        PB_ATT, PB_SU, PB_O = (2, 3), (0, 1), (4, 5)

        def k_part(i):
            c0 = 128 * i
            for hh in range(4):
                tr(pb7[:, 128 * hh:128 * hh + 128], kh_r[:, hh, c0:c0 + 128], identb[:], [Bkh[hh], Bid], [Bpb[7]])
            for p in range(2):
                tr(pb7[:, 512 + 128 * p:512 + 128 * p + 128], kh_g[:, p, c0:c0 + 128], identb[:], [Bkhg[p], Bid], [Bpb[7]])

        def khT_copy(i):
            kk = i % NKHT
            cp("act", khT[kk][:], pb7[:, 0:768], [Bpb[7]], [BkhT[kk]])

        def a1_part(i):
            c0 = 128 * i
            kind = chs[i][0]
            mask = cst[:, C_M01:C_M01 + 128] if kind == "p" else cst[:, C_MSM:C_MSM + 128]
            m4 = mask.unsqueeze(1).broadcast_to([128, 4, 128])
            am, Bam = attm[i % 2], Battm[i % 2]
            for hh in range(4):
                mm(pb[PB_ATT[0]][:, 128 * hh:128 * hh + 128], kt_r[:, hh, c0:c0 + 128], qt_r[:, hh, c0:c0 + 128], True, True, [Bkt[hh], Bqt[hh]], [Bpb[PB_ATT[0]]])
            for g in range(4):
                p = g // 2
                mm(pb[PB_ATT[1]][:, 128 * g:128 * g + 128], ktAB[:, g, c0:c0 + 128], qt_g[:, p, c0:c0 + 128], True, True, [Bktab[p], Bqtg[p]], [Bpb[PB_ATT[1]]])
            tt("dve", am[:, 0:512].rearrange("p (a b) -> p a b", b=128), pb[PB_ATT[0]][:].rearrange("p (a b) -> p a b", b=128), m4, MUL, [Bpb[PB_ATT[0]], Bcst], [Bam])
            tt("dve", am[:, 512:1024].rearrange("p (a b) -> p a b", b=128), pb[PB_ATT[1]][:].rearrange("p (a b) -> p a b", b=128), m4, MUL, [Bpb[PB_ATT[1]], Bcst], [Bam])

        def su_mm(i, kt_, Bk_, sub):
            for hh in range(4):
                mm(pb[sub[0]][:, 128 * hh:128 * hh + 128], kt_[:, 128 * hh:128 * hh + 128], v_tok[:, i, 128 * hh:128 * hh + 128], True, True, [Bk_, Bv[i]], [Bpb[sub[0]]])
            for p in range(2):
                mm(pb[sub[1]][:, 256 * p:256 * p + 256], kt_[:, 512 + 128 * p:512 + 128 * p + 128], v_tok[:, i, 512 + 256 * p:512 + 256 * p + 256], True, True, [Bk_, Bv[i]], [Bpb[sub[1]]])

        def s_upd(st, sub, dec, ecol):
            for hh in range(4):
                stt(st.rf[:, hh, :], st.rf[:, hh, :], float(dec[hh]), pb[sub[0]][:, 128 * hh:128 * hh + 128], MUL, ADD, [st.Brf[hh], Bpb[sub[0]]], [st.Brf[hh]])
            for p in range(2):
                for e in range(2):
                    r0 = 64 * e
                    stt(st.gf[r0:r0 + 64, p, :], st.gf[r0:r0 + 64, p, :], E1[r0:r0 + 64, p, ecol:ecol + 1],
                        pb[sub[1]][r0:r0 + 64, 256 * p + 128 * e:256 * p + 128 * e + 128], MUL, ADD, [st.Bgf[2 * p + e], BE1[p], Bpb[sub[1]]], [st.Bgf[2 * p + e]])

        def s_cast(st, eng="act"):
            cp(eng, st.rb[:], st.rf[:], st.Brf, [st.Brb])
            gbv = st.gb[:].rearrange("p (a e) v -> p a e v", e=2)
            cp("act", gbv[0:64, :, 0, :], st.gf[0:64, :, :], st.Bgf, [st.Bgb])
            cp("act", gbv[64:128, :, 1, :], st.gf[64:128, :, :], st.Bgf, [st.Bgb])

        def o_intra(i):
            am, Bam = attm[i % 2], Battm[i % 2]
            for hh in range(8):
                bk = PB_O[hh // 4]
                oc = 128 * (hh % 4)
                mm(pb[bk][:, oc:oc + 128], v_tok[:, i, 128 * hh:128 * hh + 128], am[:, 128 * hh:128 * hh + 128], hh % 4 == 0, True, [Bv[i], Bam], [Bpb[bk]], sgc=True)

        def inter(i, s0, sn, st):
            c0 = 128 * i
            for hh in range(8):
                bk = PB_O[hh // 4]
                oc = 128 * (hh % 4)
                if hh < 4:
                    mm(pb[bk][:, oc + s0:oc + s0 + sn], st.rb[:, hh, :], qt_r[:, hh, c0 + s0:c0 + s0 + sn], False, True, [st.Brb, Bqt[hh]], [Bpb[bk]], sgc=True)
                else:
                    g = hh - 4
                    mm(pb[bk][:, oc + s0:oc + s0 + sn], st.gb[:, g, :], qt_g[:, g // 2, c0 + s0:c0 + s0 + sn], False, True, [st.Bgb, Bqtg[g // 2]], [Bpb[bk]], sgc=True)

        def o_evac(i):
            q = i % 2
            for grp in range(2):
                sl = slice(512 * grp, 512 * grp + 512)
                act(osq[q][:, sl], pb[PB_O[grp]][:], AF.Square, [Bpb[PB_O[grp]]], [Bosq[q][grp]])
                cp("dve", osb[q][:, sl], pb[PB_O[grp]][:], [Bpb[PB_O[grp]]], [Bosb[q][grp]])

        def norm_part(i):
            c0 = 128 * i
            q = i % 2
            for grp in range(2):
                sl = slice(512 * grp, 512 * grp + 512)
                r3 = rinv[:, sl].rearrange("p (a b) -> p a b", b=128)
                mm(pb[6][:], onesb[:], osq[q][:, sl], True, True, [Bones, Bosq[q][grp]], [Bpb[6]])
                act(rinv[:, sl], pb[6][:], AF.Ln, [Bpb[6]], [Brinv[grp]], scale=1.0 / 128.0, bias=EPS)
                act(rinv[:, sl], rinv[:, sl], AF.Exp, [Brinv[grp]], [Brinv[grp]], scale=-0.5)
                tt("pool", r3, r3, sg[:, 4 * grp:4 * grp + 4, c0:c0 + 128], MUL, [Brinv[grp]] + Bsg[4 * grp:4 * grp + 4], [Brinv[grp]])
                tt("pool", hT[:, 4 * grp:4 * grp + 4, c0:c0 + 128], osb[q][:, sl].rearrange("p (a b) -> p a b", b=128), r3, MUL, [Bosb[q][grp], Brinv[grp]], [Bh[fc_][i] for fc_ in range(4 * grp, 4 * grp + 4)])

        k_part(0)
        if chs[0][0] == "p":
            khT_copy(0)
        a1_part(0)
        for i, (kind, idx) in enumerate(chs):
            c0 = 128 * i
            if kind == "p":
                st = states[0]
                kk = i % NKHT
                su_mm(i, khT[kk], BkhT[kk], PB_SU)
                o_intra(i)
                if idx > 0:
                    inter(i, 0, 128, st)
                s_upd(st, PB_SU, [g_ ** 128 for g_ in GAM], c0 + 127)
                o_evac(i)
                if i + 1 < nch:
                    k_part(i + 1)
                    if chs[i + 1][0] == "p":
                        khT_copy(i + 1)
                    a1_part(i + 1)
                if idx < 15:
                    s_cast(st)
                else:
                    out_dmas.append(dma("sp", srp_d.rearrange("(h d) v -> d h v", d=128), st.rf[:], st.Brf, (), st.Brf[0]))
                    out_dmas.append(dma("sp", sgp_d.rearrange("(p q) v -> q p v", q=128), st.gf[:], st.Bgf, (), st.Bgf[0]))
                norm_part(i)
            else:
                loaded = {}

                def load_state(s, st):
                    if s in loaded:
                        return
                    loaded[s] = True
                    dma("pool", st.rf[:], sret_d[s].rearrange("(h d) v -> d h v", d=128), (), st.Brf, st.Brf[0])
                    dma("pool", st.gf[:], sgla_d[s].rearrange("(p q) v -> q p v", q=128), (), st.Bgf, st.Bgf[0])

                for s in range(NSS):
                    load_state(s, states[1 + s])
                o_intra(i)
                dec8 = [g_ ** 8 for g_ in GAM]
                for s in range(NSQ):
                    st = states[1 + s % NSS]
                    ecol = c0 + DSEQ * s + DSEQ - 1
                    s_cast(st, "act")
                    inter(i, DSEQ * s, DSEQ, st)
                    kk = khT_i[0] % NKHT
                    khT_i[0] += 1
                    kt_, Bk_ = khT[kk], BkhT[kk]
                    act(kt_[:], pb7[:, 0:768], AF.Copy, [Bpb[7], Bcst], [Bk_], scale=cst[:, C_MSEL + s:C_MSEL + s + 1])
                    sub = [PB_SU, PB_ATT][s % 2]
                    su_mm(i, kt_, Bk_, sub)
                    s_upd(st, sub, dec8, ecol)
                    out_dmas.append(dma("sp", srs_d[s].rearrange("(h d) v -> d h v", d=128), st.rf[:], st.Brf, (), st.Brf[0]))
                    out_dmas.append(dma("sp", sgs_d[s].rearrange("(p q) v -> q p v", q=128), st.gf[:], st.Bgf, (), st.Bgf[0]))
                    if s + NSS < NSQ:
                        load_state(s + NSS, states[1 + (s + NSS) % NSS])
                o_evac(i)
                norm_part(i)

        _stage(3)
        pend = []

        def flush_pend():
            while pend:
                (sbk_, n_, q_, m_) = pend.pop(0)
                mm(pb[sbk_][:, 0:n_], onesb[:], xsq[q_][:, 0:n_], m_ == 0, m_ == 7, [Bones, Bxsq[q_]], [Bpb[sbk_]])

        def resid(bk, m, c0, n, gtoff):
            flush_pend()
            if c0 < NP:
                stt(xT[:, m, c0:c0 + n], pb[bk][:, 0:n], modT[:, gtoff + m, 0:1], xT[:, m, c0:c0 + n], MUL, ADD, [Bpb[bk], Bmod] + bx(m, c0, n), bx(m, c0, n))
            else:
                tm, Btm = tmp()
                tt("dve", v3(tm[:, 0:n]), v3(pb[bk][:, 0:n]), bc_seq(modT[:, gtoff + m, :]), MUL, [Bpb[bk], Bmod], [Btm])
                tt("pool", xT[:, m, c0:c0 + n], xT[:, m, c0:c0 + n], tm[:, 0:n], ADD, [Btm] + bx(m, c0, n), bx(m, c0, n))
            q = m % 2
            sbk = 6 if c0 < NP else 5
            act(xsq[q][:, 0:n], xT[:, m, c0:c0 + n], AF.Square, bx(m, c0, n), [Bxsq[q]])
            pend.append((sbk, n, q, m))

        def finish_stats():
            flush_pend()
            for (c0, n) in sbs:
                sbk = 6 if c0 < NP else 5
                act(rsb[:, c0:c0 + n], pb[sbk][:, 0:n], AF.Ln, [Bpb[sbk]], brs(c0, n), scale=1.0 / D, bias=EPS)
                act(rsb[:, c0:c0 + n], rsb[:, c0:c0 + n], AF.Exp, brs(c0, n), brs(c0, n), scale=-0.5)

        pbanks[:] = [0, 1, 2, 3, 4]
        for t_ in range(4):
            wt, Bw = wget()
            for ch in range(2):
                m = 2 * t_ + ch
                for (c0, n) in sbs:
                    bk = nbank()
                    for kc in range(8):
                        mm(pb[bk][:, 0:n], wt[:, kc, ch * 128:ch * 128 + 128], hT[:, kc, c0:c0 + n], kc == 0, kc == 7, [Bw] + hb(kc, c0, n), [Bpb[bk]])
                    resid(bk, m, c0, n, 16)

        def stats():
            for (c0, n) in sbs:
                for fc in range(8):
                    q = fc % 2
                    act(xsq[q][:, 0:n], xT[:, fc, c0:c0 + n], AF.Square, bx(fc, c0, n), [Bxsq[q]])
                    mm(pb[6][:, 0:n], onesb[:], xsq[q][:, 0:n], fc == 0, fc == 7, [Bones, Bxsq[q]], [Bpb[6]])
                act(rsb[:, c0:c0 + n], pb[6][:, 0:n], AF.Ln, [Bpb[6]], brs(c0, n), scale=1.0 / D, bias=EPS)
                act(rsb[:, c0:c0 + n], rsb[:, c0:c0 + n], AF.Exp, brs(c0, n), brs(c0, n), scale=-0.5)

        _stage(4)
        finish_stats()
        modulate(g2, 24)

        P.alias(Bhid, mixer_bufs)
        fb = [(0, 1), (2, 3), (4, 5)]
        fbi = 0
        for j in range(11):
            wa, Bwa = wget()
            wb_, Bwb = wget()
            for ch in range(2):
                k = 2 * j + ch
                for (c0, n) in sbs:
                    ba, bb = fb[fbi % 3]
                    fbi += 1
                    for kc in range(8):
                        mm(pb[ba][:, 0:n], wa[:, kc, ch * 128:ch * 128 + 128], hT[:, kc, c0:c0 + n], kc == 0, kc == 7, [Bwa] + hb(kc, c0, n), [Bpb[ba]])
                    for kc in range(8):
                        mm(pb[bb][:, 0:n], wb_[:, kc, ch * 128:ch * 128 + 128], hT[:, kc, c0:c0 + n], kc == 0, kc == 7, [Bwb] + hb(kc, c0, n), [Bpb[bb]])
                    tm, Btm = tmp()
                    act(tm[:, 0:n], pb[ba][:, 0:n], AF.Silu, [Bpb[ba]], [Btm])
                    tt("dve", hid[:, k, c0:c0 + n], tm[:, 0:n], pb[bb][:, 0:n], MUL, [Btm, Bpb[bb]], [Bhid[k]])

        if bi + 1 < len(BLOCKS):
            x_dma(bi + 1, 0)
            x_dma(bi + 1, 1)
        _stage(5)
        for m in range(8):
            w0, Bw0 = wget()
            w1, Bw1 = wget()
            for (c0, n) in sbs:
                bk = nbank()
                for kc in range(NKF):
                    w_, Bw_ = (w0, Bw0) if kc < 11 else (w1, Bw1)
                    mm(pb[bk][:, 0:n], w_[:, kc % 11, :], hid[:, kc, c0:c0 + n], kc == 0, kc == NKF - 1, [Bw_, Bhid[kc]], [Bpb[bk]])
                resid(bk, m, c0, n, 40)

        _stage(6)
        finish_stats()
        nxt = bi + 1 if bi + 1 < len(BLOCKS) else None
        if nxt is not None:
            p1a_begin(nxt)
        nnx = len(BLOCKS[nxt]) if nxt is not None else 0
        def y_mul(i):
            c0 = 128 * i
            q = i % 2
            for fc in range(8):
                stt(yT[q][:, fc, :], xT[:, fc, c0:c0 + 128], vc[:, V_FINN + fc:V_FINN + fc + 1], rsb[:, c0:c0 + 128], MUL, MUL, bx(fc, c0, 128) + [Bvc] + brs(c0, 128), ByT[q])

        y_mul(0)
        for i in range(max(nch, nnx)):
            if i < nch:
                kind, idx = chs[i]
                q = i % 2
                bks = [(4, 5), (2, 3)][q]
                for fc in range(8):
                    bk = bks[fc // 4]
                    tr(pb[bk][:, (fc % 4) * 128:(fc % 4) * 128 + 128], yT[q][:, fc, :], ident, ByT[q] + [Bcst], [Bpb[bk]])
                if i + 1 < nch:
                    y_mul(i + 1)
                ta, Ba = tmp()
                tb_, Bb = tmp()
                cp("act", ta[:], pb[bks[0]][:], [Bpb[bks[0]]], [Ba])
                cp("dve", tb_[:], pb[bks[1]][:], [Bpb[bks[1]]], [Bb])
                dst = yp_d[128 * idx:128 * idx + 128, :] if kind == "p" else ys_d
                out_dmas.append(dma("sp", dst[:, 0:512], ta[:], [Ba], (), Ba))
                out_dmas.append(dma("sp", dst[:, 512:1024], tb_[:], [Bb], (), Bb))
            if i < nnx:
                p1a_chunk(nxt, i)
                if i + 2 < nnx:
                    x_dma(nxt, i + 2)

    try:
        _stage(0)
        for bi in range(len(BLOCKS)):
            run_block(bi)
    except _Stop:
        pass
    if KSTAGE < 99:
        dbg_h = dout("dbg_h", [128, 8, TBMAX])
        dbg_x = dout("dbg_x", [128, 8, TBMAX])
        Bd1, Bd2 = P.buf("dbg1"), P.buf("dbg2")
        dma("pool", dbg_h, hT[:], [b_ for r_ in Bh for b_ in r_], (), Bd1)
        dma("sp", dbg_x, xT[:], [b_ for r_ in BxT for b_ in r_], (), Bd2)

    P.op("sp", lambda h: h.nop(), extra=all_dmas)
    P.emit(es)
    es.close()
    return nc


_NC = None


def kernel(x_prompt, x_sample, state_ret, state_gla, c_prompt, c_sample,
           w_ada, b_ada, mix_norm, w_in, w_gk_up, b_gk_up, ret_norm, gla_norm,
           w_out, ffn_norm, w_gate_up, w_down, final_norm):
    global _NC
    f = lambda a: np.ascontiguousarray(np.asarray(a, dtype=np.float32))
    x_prompt, x_sample, state_ret, state_gla = f(x_prompt), f(x_sample), f(state_ret), f(state_gla)
    c_prompt, c_sample = f(c_prompt), f(c_sample)
    cst, cossw = host_consts()
    vecs = np.concatenate([f(mix_norm).reshape(8, 128), f(ffn_norm).reshape(8, 128), f(final_norm).reshape(8, 128),
                           f(ret_norm).reshape(4, 128), f(gla_norm).reshape(4, 128), f(b_gk_up).reshape(2, 128),
                           f(b_ada).reshape(48, 128)], axis=0)
    shared = {"w_ada": f(w_ada)[0], "w_in": f(w_in)[0], "w_gk": f(w_gk_up)[0], "w_out": f(w_out)[0],
              "w_gu": f(w_gate_up)[0], "w_dn": f(w_down)[0], "vecs": np.ascontiguousarray(vecs), "cst": cst, "cossw": cossw}
    in_maps = []
    for r in range(NCORES):
        m = dict(shared)
        m["xp"] = x_prompt[r]
        m["xs"] = np.ascontiguousarray(x_sample[NSQ * r:NSQ * (r + 1)].reshape(128, D))
        m["c17"] = np.ascontiguousarray(np.concatenate([c_prompt[r:r + 1], c_sample[NSQ * r:NSQ * (r + 1)]], axis=0))
        m["sret"] = np.ascontiguousarray(state_ret[0, NSQ * r:NSQ * (r + 1)].reshape(NSQ, 512, 128))
        m["sgla"] = np.ascontiguousarray(state_gla[0, NSQ * r:NSQ * (r + 1)].reshape(NSQ, 256, 128))
        in_maps.append(m)
    if _NC is None:
        _NC = build()
    res = run_bass_kernel_spmd(_NC, in_maps, core_ids=list(range(NCORES)))
    rs = res.results
    if KSTAGE < 99:
        return rs
    y_prompt = np.stack([rs[r]["yp"] for r in range(NCORES)], axis=0).astype(np.float32)
    y_sample = np.concatenate([rs[r]["ys"].reshape(NSQ, DSEQ, D) for r in range(NCORES)], axis=0).astype(np.float32)
    srp = np.stack([rs[r]["srp"].reshape(4, 128, 128) for r in range(NCORES)], axis=0)[None].astype(np.float32)
    sgp = np.stack([rs[r]["sgp"].reshape(4, 64, 128) for r in range(NCORES)], axis=0)[None].astype(np.float32)
    srs = np.concatenate([rs[r]["srs"].reshape(NSQ, 4, 128, 128) for r in range(NCORES)], axis=0)[None].astype(np.float32)
    sgs = np.concatenate([rs[r]["sgs"].reshape(NSQ, 4, 64, 128) for r in range(NCORES)], axis=0)[None].astype(np.float32)
    return (y_prompt, y_sample, srp, sgp, srs, sgs)
```

```python
import numpy as np
from contextlib import ExitStack
import concourse.bass as bass
import concourse.mybir as mybir
from concourse.bass_utils import run_bass_kernel_spmd

F32 = mybir.dt.float32
BF16 = mybir.dt.bfloat16
AF = mybir.ActivationFunctionType
ALU = mybir.AluOpType
MUL, ADD = ALU.mult, ALU.add

ENGS = ("pe", "act", "dve", "pool", "sp")
import os
KSTAGE = int(os.environ.get("KSTAGE", "99"))


class _Stop(Exception):
    pass


def _stage(n):
    if KSTAGE == n:
        raise _Stop()


class Buf:
    __slots__ = ("name", "w", "r", "rd", "sem", "cnt", "extra")

    def __init__(self, name):
        self.name = name
        self.w = None
        self.r = {}
        self.rd = []
        self.sem = None
        self.cnt = 0
        self.extra = []


class Op:
    __slots__ = ("eng", "fn", "deps", "sig", "dma", "dbuf", "val", "sidx")

    def __init__(self, eng, fn, dma, dbuf):
        self.eng = eng
        self.fn = fn
        self.deps = []
        self.sig = False
        self.dma = dma
        self.dbuf = dbuf
        self.val = 0
        self.sidx = 0


class Prog:
    def __init__(self, nc):
        self.nc = nc
        self.ops = {e: [] for e in ENGS}
        self.all = []
        self.nbuf = 0

    def buf(self, name=None):
        self.nbuf += 1
        return Buf(name or f"b{self.nbuf}")

    def bufs(self, n, name="b"):
        return [self.buf(f"{name}{i}") for i in range(n)]

    def alias(self, dst, src):
        ops = []
        for s in src:
            if s.w is not None:
                ops.append(s.w)
            ops.extend(s.r.values())
            ops.extend(s.rd)
            ops.extend(s.extra)
        for d in dst:
            d.extra.extend(ops)

    def op(self, eng, fn, reads=(), writes=(), dma=False, dbuf=None, extra=()):
        o = Op(eng, fn, dma, dbuf)
        if dma:
            qt = "sw" if eng == "pool" else "hw"
            if dbuf.sem is None:
                dbuf.sem = {}
            ent = dbuf.sem.setdefault(qt, [None, 0])
            ent[1] += 16
            o.val = ent[1]
            o.sidx = qt
            o.sig = True
        deps = {}

        def add(d, raw):
            if d is None:
                return
            if (not d.dma) and (not dma) and d.eng == eng:
                if eng == "pe" or not raw:
                    return
            deps[id(d)] = d

        for b in reads:
            add(b.w, True)
        for b in writes:
            add(b.w, False)
            for r in b.r.values():
                add(r, False)
            for r in b.rd:
                add(r, False)
            for r in b.extra:
                add(r, False)
        for d in extra:
            deps[id(d)] = d
        o.deps = list(deps.values())
        for d in o.deps:
            d.sig = True
        for b in writes:
            b.w = o
            b.r = {}
            b.rd = []
            b.extra = []
        for b in reads:
            if dma:
                b.rd.append(o)
            else:
                b.r[eng] = o
        self.ops[eng].append(o)
        self.all.append(o)
        return o

    def emit(self, es):
        nc = self.nc
        esem = {e: es.enter_context(nc.semaphore(f"s_{e}")) for e in ENGS}
        nsem = len(ENGS)
        for o in self.all:
            if o.dma and o.dbuf.sem[o.sidx][0] is None:
                o.dbuf.sem[o.sidx][0] = es.enter_context(nc.semaphore(f"d{o.sidx}_{o.dbuf.name}"))
                nsem += 1
        self.nsem = nsem
        for e in ENGS:
            c = 0
            for o in self.ops[e]:
                if o.sig and not o.dma:
                    c += 1
                    o.sidx = c
        block = es.enter_context(nc.Block())

        def run(e, handle):
            seen = {}
            for o in self.ops[e]:
                ws = {}
                for d in o.deps:
                    if d.dma:
                        s, v = d.dbuf.sem[d.sidx][0], d.val
                    else:
                        s, v = esem[d.eng], d.sidx
                    k = id(s)
                    if seen.get(k, 0) >= v:
                        continue
                    if k not in ws or ws[k][1] < v:
                        ws[k] = (s, v)
                for k, (s, v) in ws.items():
                    handle.wait_ge(s, v)
                    seen[k] = v
                ins = o.fn(handle)
                if o.dma:
                    ins.then_inc(o.dbuf.sem[o.sidx][0], 16)
                elif o.sig:
                    ins.then_inc(esem[e], 1)

        @block.tensor
        def _(h):
            run("pe", h)

        @block.scalar
        def _(h):
            run("act", h)

        @block.vector
        def _(h):
            run("dve", h)

        @block.gpsimd
        def _(h):
            run("pool", h)

        @block.sync
        def _(h):
            run("sp", h)


D = 1024
NFC = 8
SEQ = 2048
NSQ = 16
DSEQ = 8
PAST = 16384
DFF = 2816
NKF = 22
INDIM = 3600
EPS = 1e-6
NCORES = int(os.environ.get('KCORES', '8'))
BLOCKS = [[("p", 0), ("p", 1), ("p", 2), ("p", 3), ("s", 0)],
          [("p", 4), ("p", 5), ("p", 6), ("p", 7)],
          [("p", 8), ("p", 9), ("p", 10), ("p", 11)],
          [("p", 12), ("p", 13), ("p", 14), ("p", 15)]]
TBMAX = 640
NTOK = SEQ + NSQ * DSEQ
GAM = [1.0 - 2.0 ** (-5 - h) for h in range(4)]

C_E1R, C_E2R, C_EHR, C_E1S, C_E2S, C_EHS = 0, 512, 1024, 1536, 2048, 2560
C_M01, C_MSM, C_MSEL, C_ID, C_SMP, C_SMS = 3072, 3200, 3328, 3344, 3472, 3600
NCR = 3728
V_MIXN, V_FFNN, V_FINN, V_RETN, V_GLAN, V_BGK, V_BADA, NV = 0, 8, 16, 24, 28, 32, 34, 82


def host_consts():
    half = 64
    inv = 10000.0 ** (-np.arange(half, dtype=np.float64) / half)
    pos_p = np.arange(SEQ, dtype=np.float64)
    pos_s = np.tile(PAST + np.arange(DSEQ, dtype=np.float64), NSQ)
    pos = np.concatenate([pos_p[:512], pos_s, pos_p[512:]])
    ang = pos[None, :] * np.concatenate([inv, inv])[:, None]
    cosT = np.cos(ang)
    sw = np.sin(ang)
    sw[64:] *= -1.0
    cossw = np.stack([cosT, sw], axis=1).astype(np.float32)
    cst = np.zeros((128, NCR), np.float64)
    t = np.arange(128, dtype=np.float64)
    ts = t % 8
    for h in range(4):
        lg = np.log1p(-2.0 ** (-5 - h))
        cst[:, C_E1R + 128 * h:C_E1R + 128 * h + 128] = np.exp(lg * (t + 1))[None]
        cst[:, C_E2R + 128 * h:C_E2R + 128 * h + 128] = (np.exp(-lg * (t + 1)) * 128 ** -0.5)[None]
        cst[:, C_EHR + 128 * h:C_EHR + 128 * h + 128] = (np.exp(lg * (127 - t)) * 128 ** -0.5)[None]
        cst[:, C_E1S + 128 * h:C_E1S + 128 * h + 128] = np.exp(lg * (ts + 1))[None]
        cst[:, C_E2S + 128 * h:C_E2S + 128 * h + 128] = (np.exp(-lg * (ts + 1)) * 128 ** -0.5)[None]
        cst[:, C_EHS + 128 * h:C_EHS + 128 * h + 128] = (np.exp(lg * (7 - ts)) * 128 ** -0.5)[None]
    j = np.arange(128)[:, None]
    i = np.arange(128)[None, :]
    cst[:, C_M01:C_M01 + 128] = (i >= j)
    cst[:, C_MSM:C_MSM + 128] = (i >= j) & ((i // 8) == (j // 8))
    cst[:, C_MSEL:C_MSEL + 16] = ((j // 8) == np.arange(16)[None, :])
    cst[:, C_ID:C_ID + 128] = (i == j)
    cst[:, C_SMP:C_SMP + 128] = (i != 0)
    cst[:, C_SMS:C_SMS + 128] = ((i % 8) != 0)
    return cst.astype(np.float32), cossw


def build():
    nc = bass.Bass("TRN2", target_bir_lowering=False)

    def din(name, shape):
        return nc.dram_tensor(name, list(shape), F32, kind="ExternalInput").ap()

    def dout(name, shape):
        return nc.dram_tensor(name, list(shape), F32, kind="ExternalOutput").ap()

    xp_d = din("xp", [SEQ, D])
    xs_d = din("xs", [128, D])
    c_d = din("c17", [17, D])
    sret_d = din("sret", [NSQ, 512, 128])
    sgla_d = din("sgla", [NSQ, 256, 128])
    wada_d = din("w_ada", [D, 6 * D])
    win_d = din("w_in", [D, INDIM])
    wgk_d = din("w_gk", [16, 256])
    wout_d = din("w_out", [D, D])
    wgu_d = din("w_gu", [D, 2 * DFF])
    wdn_d = din("w_dn", [DFF, D])
    vec_d = din("vecs", [NV, 128])
    cst_d = din("cst", [128, NCR])
    cs_d = din("cossw", [128, 2, NTOK])
    yp_d = dout("yp", [SEQ, D])
    ys_d = dout("ys", [128, D])
    srp_d = dout("srp", [512, 128])
    sgp_d = dout("sgp", [256, 128])
    srs_d = dout("srs", [NSQ, 512, 128])
    sgs_d = dout("sgs", [NSQ, 256, 128])

    es = ExitStack()
    P = Prog(nc)

    def sb(name, shape, dt=F32):
        return es.enter_context(nc.sbuf_tensor("sb_" + name, list(shape), dt))

    pb = [es.enter_context(nc.psum_tensor(f"pb{i}", [128, 512], F32)) for i in range(7)]
    pb7 = es.enter_context(nc.psum_tensor("pb7", [128, 1024], BF16))
    Bpb = P.bufs(8, "pb")

    def mm(out, lhsT, rhs, start, stop, reads, writes, sgc=False):
        return P.op("pe", lambda h: h.matmul(out, lhsT=lhsT, rhs=rhs, start=start, stop=stop, skip_group_check=sgc), reads, writes)

    def tr(out, in_, ident, reads, writes):
        return P.op("pe", lambda h: h.transpose(out=out, in_=in_, identity=ident), reads, writes)

    def act(out, in_, func, reads, writes, **kw):
        return P.op("act", lambda h: h.activation(out=out, in_=in_, func=func, **kw), reads, writes)

    def tt(eng, out, in0, in1, op, reads, writes):
        return P.op(eng, lambda h: h.tensor_tensor(out=out, in0=in0, in1=in1, op=op), reads, writes)

    def stt(out, in0, scalar, in1, op0, op1, reads, writes):
        return P.op("dve", lambda h: h.scalar_tensor_tensor(out=out, in0=in0, scalar=scalar, in1=in1, op0=op0, op1=op1), reads, writes)

    def tsc(eng, out, in0, s1, s2, op0, op1, reads, writes):
        if s2 is None and eng == "pool":
            s2, op1 = 0.0, ADD
        if s2 is None:
            return P.op(eng, lambda h: h.tensor_scalar(out=out, in0=in0, scalar1=s1, scalar2=None, op0=op0), reads, writes)
        return P.op(eng, lambda h: h.tensor_scalar(out=out, in0=in0, scalar1=s1, scalar2=s2, op0=op0, op1=op1), reads, writes)

    def cp(eng, out, in_, reads, writes):
        if eng == "act":
            return act(out, in_, AF.Copy, reads, writes)
        return P.op(eng, lambda h: h.tensor_copy(out=out, in_=in_), reads, writes)

    all_dmas = []

    def dma(q, out, in_, reads, writes, dbuf):
        o = P.op(q, lambda h: h.dma_start(out=out, in_=in_), reads, writes, dma=True, dbuf=dbuf)
        all_dmas.append(o)
        return o

    def mset(eng, ap, val, writes):
        return P.op(eng, lambda h: h.memset(ap, val), (), writes)

    out_dmas = []

    cst = sb("cst", [128, NCR])
    Bcst = P.buf("cst")
    identb = sb("identb", [128, 128], BF16)
    onesb = sb("onesb", [128, 128], BF16)
    onesf = sb("onesf", [128, 128])
    Bid = P.buf("idb")
    Bones = P.buf("ones")
    vin = sb("vin", [NV, 128])
    vc = sb("vc", [128, NV])
    nbgk = sb("nbgk", [128, 2])
    Bvin, Bvc = P.buf("vin"), P.buf("vc")
    cT = sb("cT", [128, 8, 17], BF16)
    BcT = P.buf("cT")
    modT = sb("modT", [128, 48, 17])
    g1 = sb("g1", [128, 8, 17])
    g2 = sb("g2", [128, 8, 17])
    Bmod = P.buf("mod")
    wgk = sb("wgk", [16, 256], BF16)
    Bwgk = P.buf("wgk")
    smask = sb("smask", [128, TBMAX])
    Bsmask = P.buf("smask")

    ident = cst[:, C_ID:C_ID + 128]

    NSLOT = 5
    wslots = [sb(f"wslot{i}", [128, 2048], BF16) for i in range(NSLOT)]
    Bws = P.bufs(NSLOT, "ws")

    xT = sb("xT", [128, 8, TBMAX])
    BxT = [[P.buf(f"xT{fc}_{i}") for i in range(5)] for fc in range(8)]

    def bx(fc, c0, n):
        return BxT[fc][c0 // 128:(c0 + n) // 128]
    hT = sb("hT", [128, 8, TBMAX], BF16)
    Bh = [[P.buf(f"h{fc}_{i}") for i in range(5)] for fc in range(8)]
    cossw = sb("cossw", [128, 2, TBMAX])
    Bcs = P.buf("cossw")
    rsb = sb("rsb", [128, TBMAX])
    Brsbs = P.bufs(5, "rsb")

    def brs(c0, n):
        return Brsbs[c0 // 128:(c0 + n) // 128]
    A_QT, A_KT, A_KH = 0, 4 * TBMAX, 8 * TBMAX
    A_V = 12 * TBMAX
    A_SG = A_V + 5 * 1024
    A_QTG = A_SG + 8 * TBMAX
    A_KTAB = A_QTG + 2 * TBMAX
    A_KHG = A_KTAB + 4 * TBMAX
    A_END = A_KHG + 2 * TBMAX
    assert A_END >= NKF * TBMAX
    arena = sb("arena", [128, A_END], BF16)

    def aview(off, n, w):
        return arena[:, off:off + n * w].rearrange("p (a b) -> p a b", b=w)

    qt_r, kt_r, kh_r = aview(A_QT, 4, TBMAX), aview(A_KT, 4, TBMAX), aview(A_KH, 4, TBMAX)
    v_tok = aview(A_V, 5, 1024)
    sg = aview(A_SG, 8, TBMAX)
    qt_g, ktAB, kh_g = aview(A_QTG, 2, TBMAX), aview(A_KTAB, 4, TBMAX), aview(A_KHG, 2, TBMAX)
    hid = aview(0, NKF, TBMAX)
    Bqt, Bkt, Bkh = P.bufs(4, "qt"), P.bufs(4, "kt"), P.bufs(4, "kh")
    Bv = P.bufs(5, "v")
    Bsg = P.bufs(8, "sg")
    Bqtg, Bktab, Bkhg = P.bufs(2, "qtg"), P.bufs(2, "ktab"), P.bufs(2, "khg")
    Bhid = P.bufs(NKF, "hid")
    mixer_bufs = Bqt + Bkt + Bkh + Bv + Bsg + Bqtg + Bktab + Bkhg

    lrT = sb("lrT", [16, TBMAX], BF16)
    Blr = P.buf("lr")
    spb = sb("spb", [128, 2, TBMAX])
    bsum = sb("bsum", [128, 2, TBMAX])
    E1, E2 = bsum, spb
    Bspb, Bbsum = P.bufs(2, "spb"), P.bufs(2, "bsum")
    BE1, BE2 = Bbsum, Bspb
    NTMP = 6
    tmps = [sb(f"tmp{i}", [128, 512]) for i in range(NTMP)]
    Btmp = P.bufs(NTMP, "tmp")
    tmp_i = [0]

    def tmp():
        k = tmp_i[0] % NTMP
        tmp_i[0] += 1
        return tmps[k], Btmp[k]

    xin = [sb(f"xin{i}", [128, D]) for i in range(2)]
    Bxin = P.bufs(2, "xin")
    cin, Bcin = xin[1], Bxin[1]
    ssq = sb("ssq", [128, 2])
    diag = [sb(f"diag{i}", [128, 128]) for i in range(2)]
    Bssq, Bdiag = P.bufs(2, "ssq"), P.bufs(2, "diag")

    attm = [sb(f"attm{i}", [128, 1024], BF16) for i in range(2)]
    Battm = P.bufs(2, "attm")
    osq = [sb(f"osq{i}", [128, 1024], BF16) for i in range(2)]
    Bosq = [P.bufs(2, f"osq{i}") for i in range(2)]
    rinv = sb("rinv", [128, 1024])
    Brinv = P.bufs(2, "rinv")
    osb = [sb(f"osb{i}", [128, 1024]) for i in range(2)]
    Bosb = [P.bufs(2, f"osb{i}") for i in range(2)]
    sqj, Bsqj = osq[0], Bosq[0][0]
    xsq = [osq[0][:, 0:512], osq[1][:, 0:512]]
    Bxsq = [Bosq[0][0], Bosq[1][0]]
    yT = [osb[0][:].rearrange("p (a b) -> p a b", b=128), osb[1][:].rearrange("p (a b) -> p a b", b=128)]
    ByT = [Bosb[0], Bosb[1]]
    NKHT = 2
    khT = [sb(f"khT{i}", [128, 768], BF16) for i in range(NKHT)]
    BkhT = P.bufs(NKHT, "khT")
    khT_i = [0]

    NSS = 4

    class State:
        pass

    states = []
    for k in range(1 + NSS):
        st = State()
        st.rf = sb(f"Srf{k}", [128, 4, 128])
        st.gf = sb(f"Sgf{k}", [128, 2, 128])
        st.rb = sb(f"Srb{k}", [128, 4, 128], BF16)
        st.gb = sb(f"Sgb{k}", [128, 4, 128], BF16)
        st.Brf, st.Bgf, st.Brb, st.Bgb = P.bufs(4, f"Srf{k}_"), P.bufs(4, f"Sgf{k}_"), P.buf(f"Srb{k}"), P.buf(f"Sgb{k}")
        states.append(st)

    jobs = []
    NJB = 15 + 4 + 22 + 16
    wscr = nc.dram_tensor("wscr", [NJB, 128, 2048], BF16, kind="Internal").ap()
    Bscr = P.bufs(NJB, "wscr")

    def job_cols(w, c0, ncols, cidx=None, fresh=True):
        src = w[:, c0:c0 + ncols].rearrange("(kc k) c -> k kc c", k=128) if fresh else None
        return (lambda s: s[:, 0:8 * ncols].rearrange("p (k c) -> p k c", c=ncols), src, cidx, 8 * ncols)

    def job_down(m, half, cidx=None, fresh=True):
        src = wdn_d[half * 1408:(half + 1) * 1408, 128 * m:128 * m + 128].rearrange("(kc k) c -> k kc c", k=128) if fresh else None
        return (lambda s: s[:, 0:11 * 128].rearrange("p (k c) -> p k c", c=128), src, cidx, 11 * 128)

    for t_ in range(8):
        jobs.append(job_cols(wada_d, 256 * t_, 256))
    WIN_ORDER = [("lr", 3584, 16), ("v", 1024, 256), ("v", 1280, 256), ("qg", 2048, 256), ("kg", 2304, 256),
                 ("qr", 0, 256), ("gr", 1536, 256), ("qr", 256, 256), ("gr", 1792, 256),
                 ("kr", 512, 256), ("gg", 3072, 256), ("kr", 768, 256), ("gg", 3328, 256),
                 ("v", 2560, 256), ("v", 2816, 256)]
    for b_ in range(len(BLOCKS)):
        fr = b_ == 0
        ci = 0
        for k_, (_, c0, n_) in enumerate(WIN_ORDER):
            jobs.append(job_cols(win_d, c0, n_, ci, fr))
            ci += 1
            if b_ == 0:
                jobs.append(job_cols(wada_d, 256 * (8 + k_), 256))
        if b_ == 0:
            jobs.append(job_cols(wada_d, 256 * 23, 256))
        for t_ in range(4):
            jobs.append(job_cols(wout_d, 256 * t_, 256, ci, fr))
            ci += 1
        for j_ in range(11):
            jobs.append(job_cols(wgu_d, 256 * j_, 256, ci, fr))
            jobs.append(job_cols(wgu_d, DFF + 256 * j_, 256, ci + 1, fr))
            ci += 2
        for m_ in range(8):
            jobs.append(job_down(m_, 0, ci, fr))
            jobs.append(job_down(m_, 1, ci + 1, fr))
            ci += 2
        assert ci == NJB
    wstate = {"issued": 0, "next": 0}

    def wget():
        n = wstate["next"]
        wstate["next"] += 1
        while wstate["issued"] < min(n + NSLOT - 1, len(jobs)):
            k = wstate["issued"]
            vf, src, cidx, L = jobs[k]
            s = k % NSLOT
            if src is not None:
                dma("pool", vf(wslots[s]), src, (), [Bws[s]], Bws[s])
                if cidx is not None:
                    dma("sp", wscr[cidx][:, 0:L], wslots[s][:, 0:L], [Bws[s]], [Bscr[cidx]], Bws[s])
            else:
                dma("sp", wslots[s][:, 0:L], wscr[cidx][:, 0:L], [Bscr[cidx]], [Bws[s]], Bws[s])
            wstate["issued"] += 1
        vf = jobs[n][0]
        return vf(wslots[n % NSLOT]), Bws[n % NSLOT]

    dma("sp", cst[:], cst_d, (), [Bcst], Bcst)
    dma("sp", vin[:], vec_d, (), [Bvin], Bvin)
    dma("sp", cin[0:17, :], c_d, (), [Bcin], Bcin)
    dma("pool", wgk[:], wgk_d, (), [Bwgk], Bwgk)
    cp("dve", identb[:], ident, [Bcst], [Bid])
    mset("pool", onesb[:], 1.0, [Bones])
    mset("pool", onesf[:], 1.0, [Bones])
    mset("pool", arena[:, A_KTAB:A_KTAB + 4 * TBMAX], 0.0, Bktab)
    for st in states:
        mset("pool", st.gb[:], 0.0, [st.Bgb])
    mset("pool", states[0].rf[:], 0.0, states[0].Brf)
    mset("pool", states[0].gf[:], 0.0, states[0].Bgf)
    cp("pool", smask[:, 0:512].rearrange("p (c t) -> p c t", t=128),
       cst[:, C_SMP:C_SMP + 128].unsqueeze(1).broadcast_to([128, 4, 128]), [Bcst], [Bsmask])
    cp("pool", smask[:, 512:640], cst[:, C_SMS:C_SMS + 128], [Bcst], [Bsmask])
    tr(pb[2][:, 0:NV], vin[0:NV, :], cst[0:NV, C_ID:C_ID + NV], [Bvin, Bcst], [Bpb[2]])
    cp("dve", vc[:], pb[2][:, 0:NV], [Bpb[2]], [Bvc])
    tsc("dve", nbgk[:], vc[:, V_BGK:V_BGK + 2], -1.0, None, MUL, None, [Bvc], [Bvc])
    for kc in range(8):
        tr(pb[3][:, kc * 17:(kc + 1) * 17], cin[0:17, kc * 128:(kc + 1) * 128], cst[0:17, C_ID:C_ID + 17], [Bcin, Bcst], [Bpb[3]])
    act(cT[:], pb[3][:, 0:136].rearrange("p (k s) -> p k s", s=17), AF.Silu, [Bpb[3]], [BcT])
    xloaded = set()

    def x_dma(bi, i):
        if (bi, i) in xloaded:
            return
        xloaded.add((bi, i))
        kind, idx = BLOCKS[bi][i]
        sl = i % 2
        src = xp_d[128 * idx:128 * idx + 128, :] if kind == "p" else xs_d
        dma("sp", xin[sl][:], src, (), [Bxin[sl]], Bxin[sl])

    def p1a_begin(bi):
        TB = 128 * len(BLOCKS[bi])
        col0 = 0 if bi == 0 else 640 + 512 * (bi - 1)
        dma("sp", cossw[:, :, 0:TB], cs_d[:, :, col0:col0 + TB], (), [Bcs], Bcs)

    def p1a_chunk(bi, i):
        c0 = 128 * i
        sl = i % 2
        x_dma(bi, i)
        act(sqj[:], xin[sl][:], AF.Square, [Bxin[sl]], [Bsqj, Bosq[0][1], Bssq[sl]], accum_out=ssq[:, sl:sl + 1])
        tsc("dve", diag[sl][:], ident, ssq[:, sl:sl + 1], None, MUL, None, [Bcst, Bssq[sl]], [Bdiag[sl]])
        for fc in range(8):
            bk = fc // 4
            tr(pb[bk][:, (fc % 4) * 128:(fc % 4) * 128 + 128], xin[sl][:, fc * 128:fc * 128 + 128], ident, [Bxin[sl], Bcst], [Bpb[bk]])
        mm(pb[6][:, 0:128], onesf[:], diag[sl][:], True, True, [Bones, Bdiag[sl]], [Bpb[6]])
        cp("act", xT[:, 0:4, c0:c0 + 128], pb[0][:].rearrange("p (a b) -> p a b", b=128), [Bpb[0]], [BxT[fc][i] for fc in range(4)])
        cp("dve", xT[:, 4:8, c0:c0 + 128], pb[1][:].rearrange("p (a b) -> p a b", b=128), [Bpb[1]], [BxT[fc][i] for fc in range(4, 8)])
        act(rsb[:, c0:c0 + 128], pb[6][:, 0:128], AF.Ln, [Bpb[6]], brs(c0, 128), scale=1.0 / D, bias=EPS)
        act(rsb[:, c0:c0 + 128], rsb[:, c0:c0 + 128], AF.Exp, brs(c0, 128), brs(c0, 128), scale=-0.5)

    p1a_begin(0)
    x_dma(0, 0)
    x_dma(0, 1)
    for i_ in range(len(BLOCKS[0])):
        p1a_chunk(0, i_)
        if i_ + 2 < len(BLOCKS[0]):
            x_dma(0, i_ + 2)
    def ada_tile(t_):
        wt, Bw = wget()
        bank = 4 + (t_ % 2)
        for o2 in range(2):
            for kc in range(8):
                mm(pb[bank][:, o2 * 17:(o2 + 1) * 17], wt[:, kc, o2 * 128:(o2 + 1) * 128], cT[:, kc, :], kc == 0, kc == 7, [Bw, BcT], [Bpb[bank]])
        for o2 in range(2):
            oc = 2 * t_ + o2
            act(modT[:, oc, :], pb[bank][:, o2 * 17:(o2 + 1) * 17], AF.Identity, [Bpb[bank], Bvc], [Bmod], bias=vc[:, V_BADA + oc:V_BADA + oc + 1])

    for t_ in range(8):
        ada_tile(t_)
    for fc in range(8):
        tsc("dve", g1[:, fc, :], modT[:, 8 + fc, :], 1.0, vc[:, V_MIXN + fc:V_MIXN + fc + 1], ADD, MUL, [Bmod, Bvc], [Bmod])

    def ada_finish():
        for fc in range(8):
            tsc("dve", g2[:, fc, :], modT[:, 32 + fc, :], 1.0, vc[:, V_FFNN + fc:V_FFNN + fc + 1], ADD, MUL, [Bmod, Bvc], [Bmod])

    def bc_seq(ap2d):
        return ap2d[:, 1:17].unsqueeze(2).broadcast_to([128, NSQ, DSEQ])

    def v3(ap):
        return ap.rearrange("p (s t) -> p s t", t=DSEQ)

    def run_block(bi):
        chs = BLOCKS[bi]
        nch = len(chs)
        has_s = chs[-1][0] == "s"
        npr = nch - (1 if has_s else 0)
        NP = 128 * npr
        TB = 128 * nch
        sbs = [(0, NP)] + ([(NP, 128)] if has_s else [])
        col0 = 0 if bi == 0 else 640 + 512 * (bi - 1)

        def hb(kc, c0, n):
            return Bh[kc][c0 // 128:(c0 + n) // 128]


        def modulate(gX, shoff):
            for fc in range(8):
                tm, Btm = tmp()
                stt(tm[:, 0:NP], xT[:, fc, 0:NP], gX[:, fc, 0:1], rsb[:, 0:NP], MUL, MUL, bx(fc, 0, NP) + [Bmod] + brs(0, NP), [Btm])
                act(hT[:, fc, 0:NP], tm[:, 0:NP], AF.Identity, [Btm, Bmod], hb(fc, 0, NP), bias=modT[:, shoff + fc, 0:1])
                if has_s:
                    t2, Bt2 = tmp()
                    tt("dve", t2[:, 0:128], xT[:, fc, NP:TB], rsb[:, NP:TB], MUL, bx(fc, NP, 128) + brs(NP, 128), [Bt2])
                    tt("pool", v3(t2[:, 0:128]), v3(t2[:, 0:128]), bc_seq(gX[:, fc, :]), MUL, [Bt2, Bmod], [Bt2])
                    tt("pool", v3(hT[:, fc, NP:TB]), v3(t2[:, 0:128]), bc_seq(modT[:, shoff + fc, :]), ADD, [Bt2, Bmod], hb(fc, NP, 128))

        modulate(g1, 0)
        _stage(1)

        if bi > 0:
            P.alias(mixer_bufs, Bhid)
        pbanks = [0, 1, 2, 3, 4, 5]
        pbi = [0]

        def nbank():
            k = pbanks[pbi[0] % len(pbanks)]
            pbi[0] += 1
            return k

        def proj(wt, Bw, ch, evac):
            for (c0, n) in sbs:
                bk = nbank()
                for kc in range(8):
                    mm(pb[bk][:, 0:n], wt[:, kc, ch * 128:ch * 128 + 128], hT[:, kc, c0:c0 + n], kc == 0, kc == 7, [Bw] + hb(kc, c0, n), [Bpb[bk]])
                evac(bk, c0, n)

        def etab(base, hh, c0, n):
            if c0 >= NP:
                return cst[:, base + 1536 + 128 * hh:base + 1536 + 128 * hh + 128]
            return cst[:, base + 128 * hh:base + 128 * hh + 128].unsqueeze(1).broadcast_to([128, n // 128, 128])

        def ch3(ap, c0, n):
            if c0 >= NP:
                return ap
            return ap.rearrange("p (c t) -> p c t", t=128)

        def rotary(bk, c0, n, add_eng="pool"):
            t1, B1 = tmp()
            t2, B2 = tmp()
            ps = pb[bk]
            tt("dve", t1[:, 0:n], ps[:, 0:n], cossw[:, 0, c0:c0 + n], MUL, [Bpb[bk], Bcs], [B1])
            tt("dve", t2[64:128, 0:n], ps[0:64, 0:n], cossw[0:64, 1, c0:c0 + n], MUL, [Bpb[bk], Bcs], [B2])
            tt("dve", t2[0:64, 0:n], ps[64:128, 0:n], cossw[64:128, 1, c0:c0 + n], MUL, [Bpb[bk], Bcs], [B2])
            tt(add_eng, t1[:, 0:n], t1[:, 0:n], t2[:, 0:n], ADD, [B1, B2], [B1])
            return t1, B1

        def ev_qr(hh):
            def f(bk, c0, n):
                r, Br = rotary(bk, c0, n)
                tt("pool", ch3(qt_r[:, hh, c0:c0 + n], c0, n), ch3(r[:, 0:n], c0, n), etab(C_E1R, hh, c0, n), MUL, [Br, Bcst], [Bqt[hh]])
            return f

        def ev_kr(hh):
            def f(bk, c0, n):
                r, Br = rotary(bk, c0, n, "dve")
                tt("pool", ch3(kt_r[:, hh, c0:c0 + n], c0, n), ch3(r[:, 0:n], c0, n), etab(C_E2R, hh, c0, n), MUL, [Br, Bcst], [Bkt[hh]])
                tt("pool", ch3(kh_r[:, hh, c0:c0 + n], c0, n), ch3(r[:, 0:n], c0, n), etab(C_EHR, hh, c0, n), MUL, [Br, Bcst], [Bkh[hh]])
            return f

        def ev_gate(hh):
            gcol = (V_RETN + hh) if hh < 4 else (V_GLAN + hh - 4)

            def f(bk, c0, n):
                tm, Btm = tmp()
                act(tm[:, 0:n], pb[bk][:, 0:n], AF.Silu, [Bpb[bk]], [Btm])
                act(sg[:, hh, c0:c0 + n], tm[:, 0:n], AF.Copy, [Btm, Bvc], [Bsg[hh]], scale=vc[:, gcol:gcol + 1])
            return f

        def ev_qg(p):
            def f(bk, c0, n):
                stt(qt_g[:, p, c0:c0 + n], pb[bk][:, 0:n], 0.125, E1[:, p, c0:c0 + n], MUL, MUL, [Bpb[bk], BE1[p]], [Bqtg[p]])
            return f

        def ev_kg(p):
            def f(bk, c0, n):
                tm, Btm = tmp()
                tt("dve", tm[:, 0:n], pb[bk][:, 0:n], E2[:, p, c0:c0 + n], MUL, [Bpb[bk], BE2[p]], [Btm])
                cp("act", ktAB[0:64, 2 * p, c0:c0 + n], tm[0:64, 0:n], [Btm], [Bktab[p]])
                cp("act", ktAB[64:128, 2 * p + 1, c0:c0 + n], tm[64:128, 0:n], [Btm], [Bktab[p]])
                if c0 >= NP:
                    e1l = v3(E1[:, p, c0:c0 + n])[:, :, DSEQ - 1:DSEQ].broadcast_to([128, NSQ, DSEQ])
                    tt("pool", v3(kh_g[:, p, c0:c0 + n]), v3(tm[:, 0:n]), e1l, MUL, [Btm, BE1[p]], [Bkhg[p]])
                else:
                    e1l = E1[:, p, c0:c0 + n].rearrange("p (c t) -> p c t", t=128)[:, :, 127:128].broadcast_to([128, n // 128, 128])
                    tt("pool", ch3(kh_g[:, p, c0:c0 + n], c0, n), ch3(tm[:, 0:n], c0, n), e1l, MUL, [Btm, BE1[p]], [Bkhg[p]])
            return f

        vt_i = [0]
        for wk_, (kind, wc0, wn) in enumerate(WIN_ORDER):
            if bi == 0 and wk_ > 0:
                ada_tile(8 + wk_ - 1)
            wt, Bw = wget()
            if kind == "lr":
                for (c0, n) in sbs:
                    bk = nbank()
                    for kc in range(8):
                        mm(pb[bk][0:16, 0:n], wt[:, kc, 0:16], hT[:, kc, c0:c0 + n], kc == 0, kc == 7, [Bw] + hb(kc, c0, n), [Bpb[bk]])
                    cp("act", lrT[0:16, c0:c0 + n], pb[bk][0:16, 0:n], [Bpb[bk]], [Blr])
                for p in range(2):
                    for (c0, n) in sbs:
                        bk = nbank()
                        mm(pb[bk][:, 0:n], wgk[0:16, 128 * p:128 * p + 128], lrT[0:16, c0:c0 + n], True, True, [Bwgk, Blr], [Bpb[bk]])
                        tm, Btm = tmp()
                        act(tm[:, 0:n], pb[bk][:, 0:n], AF.Exp, [Bpb[bk], Bvc], [Btm], scale=-1.0, bias=nbgk[:, p:p + 1])
                        act(spb[:, p, c0:c0 + n], tm[:, 0:n], AF.Ln, [Btm], [Bspb[p]], bias=1.0)
                    P.op("dve", lambda h, p=p: h.tensor_tensor_scan(out=bsum[:, p, 0:TB], data0=smask[:, 0:TB], data1=spb[:, p, 0:TB], initial=0.0, op0=MUL, op1=ADD),
                         [Bsmask, Bspb[p]], [Bbsum[p]])
                    act(E2[:, p, 0:TB], bsum[:, p, 0:TB], AF.Exp, [Bbsum[p]], [BE2[p]], scale=1.0 / 16.0)
                    act(E1[:, p, 0:TB], bsum[:, p, 0:TB], AF.Exp, [Bbsum[p]], [BE1[p]], scale=-1.0 / 16.0)
            elif kind == "v":
                vt = vt_i[0]
                vt_i[0] += 1
                for i in range(nch):
                    bk = nbank()
                    for kc in range(8):
                        mm(pb[bk][:, 0:256], hT[:, kc, 128 * i:128 * i + 128], wt[:, kc, 0:256], kc == 0, kc == 7, [Bw, Bh[kc][i]], [Bpb[bk]])
                    cp("act", v_tok[:, i, 256 * vt:256 * vt + 256], pb[bk][:, 0:256], [Bpb[bk]], [Bv[i]])
            else:
                for ch in range(2):
                    idx = (wc0 % 512) // 128 + ch if kind in ("qr", "kr", "gr", "gg") else ch
                    if kind == "qr":
                        proj(wt, Bw, ch, ev_qr(idx))
                    elif kind == "kr":
                        proj(wt, Bw, ch, ev_kr(idx))
                    elif kind == "gr":
                        proj(wt, Bw, ch, ev_gate(idx))
                    elif kind == "gg":
                        proj(wt, Bw, ch, ev_gate(4 + idx))
                    elif kind == "qg":
                        proj(wt, Bw, ch, ev_qg(ch))
                    elif kind == "kg":
                        proj(wt, Bw, ch, ev_kg(ch))

        if bi == 0:
            ada_tile(22)
            ada_tile(23)
            ada_finish()
        _stage(2)
        PB_ATT, PB_SU, PB_O = (2, 3), (0, 1), (4, 5)

        def k_part(i):
            c0 = 128 * i
            for hh in range(4):
                tr(pb7[:, 128 * hh:128 * hh + 128], kh_r[:, hh, c0:c0 + 128], identb[:], [Bkh[hh], Bid], [Bpb[7]])
            for p in range(2):
                tr(pb7[:, 512 + 128 * p:512 + 128 * p + 128], kh_g[:, p, c0:c0 + 128], identb[:], [Bkhg[p], Bid], [Bpb[7]])

        def khT_copy(i):
            kk = i % NKHT
            cp("act", khT[kk][:], pb7[:, 0:768], [Bpb[7]], [BkhT[kk]])

        def a1_part(i):
            c0 = 128 * i
            kind = chs[i][0]
            mask = cst[:, C_M01:C_M01 + 128] if kind == "p" else cst[:, C_MSM:C_MSM + 128]
            m4 = mask.unsqueeze(1).broadcast_to([128, 4, 128])
            am, Bam = attm[i % 2], Battm[i % 2]
            for hh in range(4):
                mm(pb[PB_ATT[0]][:, 128 * hh:128 * hh + 128], kt_r[:, hh, c0:c0 + 128], qt_r[:, hh, c0:c0 + 128], True, True, [Bkt[hh], Bqt[hh]], [Bpb[PB_ATT[0]]])
            for g in range(4):
                p = g // 2
                mm(pb[PB_ATT[1]][:, 128 * g:128 * g + 128], ktAB[:, g, c0:c0 + 128], qt_g[:, p, c0:c0 + 128], True, True, [Bktab[p], Bqtg[p]], [Bpb[PB_ATT[1]]])
            tt("dve", am[:, 0:512].rearrange("p (a b) -> p a b", b=128), pb[PB_ATT[0]][:].rearrange("p (a b) -> p a b", b=128), m4, MUL, [Bpb[PB_ATT[0]], Bcst], [Bam])
            tt("dve", am[:, 512:1024].rearrange("p (a b) -> p a b", b=128), pb[PB_ATT[1]][:].rearrange("p (a b) -> p a b", b=128), m4, MUL, [Bpb[PB_ATT[1]], Bcst], [Bam])

        def su_mm(i, kt_, Bk_, sub):
            for hh in range(4):
                mm(pb[sub[0]][:, 128 * hh:128 * hh + 128], kt_[:, 128 * hh:128 * hh + 128], v_tok[:, i, 128 * hh:128 * hh + 128], True, True, [Bk_, Bv[i]], [Bpb[sub[0]]])
            for p in range(2):
                mm(pb[sub[1]][:, 256 * p:256 * p + 256], kt_[:, 512 + 128 * p:512 + 128 * p + 128], v_tok[:, i, 512 + 256 * p:512 + 256 * p + 256], True, True, [Bk_, Bv[i]], [Bpb[sub[1]]])

        def s_upd(st, sub, dec, ecol):
            for hh in range(4):
                stt(st.rf[:, hh, :], st.rf[:, hh, :], float(dec[hh]), pb[sub[0]][:, 128 * hh:128 * hh + 128], MUL, ADD, [st.Brf[hh], Bpb[sub[0]]], [st.Brf[hh]])
            for p in range(2):
                for e in range(2):
                    r0 = 64 * e
                    stt(st.gf[r0:r0 + 64, p, :], st.gf[r0:r0 + 64, p, :], E1[r0:r0 + 64, p, ecol:ecol + 1],
                        pb[sub[1]][r0:r0 + 64, 256 * p + 128 * e:256 * p + 128 * e + 128], MUL, ADD, [st.Bgf[2 * p + e], BE1[p], Bpb[sub[1]]], [st.Bgf[2 * p + e]])

        def s_cast(st, eng="act"):
            cp(eng, st.rb[:], st.rf[:], st.Brf, [st.Brb])
            gbv = st.gb[:].rearrange("p (a e) v -> p a e v", e=2)
            cp("act", gbv[0:64, :, 0, :], st.gf[0:64, :, :], st.Bgf, [st.Bgb])
            cp("act", gbv[64:128, :, 1, :], st.gf[64:128, :, :], st.Bgf, [st.Bgb])

        def o_intra(i):
            am, Bam = attm[i % 2], Battm[i % 2]
            for hh in range(8):
                bk = PB_O[hh // 4]
                oc = 128 * (hh % 4)
                mm(pb[bk][:, oc:oc + 128], v_tok[:, i, 128 * hh:128 * hh + 128], am[:, 128 * hh:128 * hh + 128], hh % 4 == 0, True, [Bv[i], Bam], [Bpb[bk]], sgc=True)

        def inter(i, s0, sn, st):
            c0 = 128 * i
            for hh in range(8):
                bk = PB_O[hh // 4]
                oc = 128 * (hh % 4)
                if hh < 4:
                    mm(pb[bk][:, oc + s0:oc + s0 + sn], st.rb[:, hh, :], qt_r[:, hh, c0 + s0:c0 + s0 + sn], False, True, [st.Brb, Bqt[hh]], [Bpb[bk]], sgc=True)
                else:
                    g = hh - 4
                    mm(pb[bk][:, oc + s0:oc + s0 + sn], st.gb[:, g, :], qt_g[:, g // 2, c0 + s0:c0 + s0 + sn], False, True, [st.Bgb, Bqtg[g // 2]], [Bpb[bk]], sgc=True)

        def o_evac(i):
            q = i % 2
            for grp in range(2):
                sl = slice(512 * grp, 512 * grp + 512)
                act(osq[q][:, sl], pb[PB_O[grp]][:], AF.Square, [Bpb[PB_O[grp]]], [Bosq[q][grp]])
                cp("dve", osb[q][:, sl], pb[PB_O[grp]][:], [Bpb[PB_O[grp]]], [Bosb[q][grp]])

        def norm_part(i):
            c0 = 128 * i
            q = i % 2
            for grp in range(2):
                sl = slice(512 * grp, 512 * grp + 512)
                r3 = rinv[:, sl].rearrange("p (a b) -> p a b", b=128)
                mm(pb[6][:], onesb[:], osq[q][:, sl], True, True, [Bones, Bosq[q][grp]], [Bpb[6]])
                act(rinv[:, sl], pb[6][:], AF.Ln, [Bpb[6]], [Brinv[grp]], scale=1.0 / 128.0, bias=EPS)
                act(rinv[:, sl], rinv[:, sl], AF.Exp, [Brinv[grp]], [Brinv[grp]], scale=-0.5)
                tt("pool", r3, r3, sg[:, 4 * grp:4 * grp + 4, c0:c0 + 128], MUL, [Brinv[grp]] + Bsg[4 * grp:4 * grp + 4], [Brinv[grp]])
                tt("pool", hT[:, 4 * grp:4 * grp + 4, c0:c0 + 128], osb[q][:, sl].rearrange("p (a b) -> p a b", b=128), r3, MUL, [Bosb[q][grp], Brinv[grp]], [Bh[fc_][i] for fc_ in range(4 * grp, 4 * grp + 4)])

        k_part(0)
        if chs[0][0] == "p":
            khT_copy(0)
        a1_part(0)
        for i, (kind, idx) in enumerate(chs):
            c0 = 128 * i
            if kind == "p":
                st = states[0]
                kk = i % NKHT
                su_mm(i, khT[kk], BkhT[kk], PB_SU)
                o_intra(i)
                if idx > 0:
                    inter(i, 0, 128, st)
                s_upd(st, PB_SU, [g_ ** 128 for g_ in GAM], c0 + 127)
                o_evac(i)
                if i + 1 < nch:
                    k_part(i + 1)
                    if chs[i + 1][0] == "p":
                        khT_copy(i + 1)
                    a1_part(i + 1)
                if idx < 15:
                    s_cast(st)
                else:
                    out_dmas.append(dma("sp", srp_d.rearrange("(h d) v -> d h v", d=128), st.rf[:], st.Brf, (), st.Brf[0]))
                    out_dmas.append(dma("sp", sgp_d.rearrange("(p q) v -> q p v", q=128), st.gf[:], st.Bgf, (), st.Bgf[0]))
                norm_part(i)
            else:
                loaded = {}

                def load_state(s, st):
                    if s in loaded:
                        return
                    loaded[s] = True
                    dma("pool", st.rf[:], sret_d[s].rearrange("(h d) v -> d h v", d=128), (), st.Brf, st.Brf[0])
                    dma("pool", st.gf[:], sgla_d[s].rearrange("(p q) v -> q p v", q=128), (), st.Bgf, st.Bgf[0])

                for s in range(NSS):
                    load_state(s, states[1 + s])
                o_intra(i)
                dec8 = [g_ ** 8 for g_ in GAM]
                for s in range(NSQ):
                    st = states[1 + s % NSS]
                    ecol = c0 + DSEQ * s + DSEQ - 1
                    s_cast(st, "act")
                    inter(i, DSEQ * s, DSEQ, st)
                    kk = khT_i[0] % NKHT
                    khT_i[0] += 1
                    kt_, Bk_ = khT[kk], BkhT[kk]
                    act(kt_[:], pb7[:, 0:768], AF.Copy, [Bpb[7], Bcst], [Bk_], scale=cst[:, C_MSEL + s:C_MSEL + s + 1])
                    sub = [PB_SU, PB_ATT][s % 2]
                    su_mm(i, kt_, Bk_, sub)
                    s_upd(st, sub, dec8, ecol)
                    out_dmas.append(dma("sp", srs_d[s].rearrange("(h d) v -> d h v", d=128), st.rf[:], st.Brf, (), st.Brf[0]))
                    out_dmas.append(dma("sp", sgs_d[s].rearrange("(p q) v -> q p v", q=128), st.gf[:], st.Bgf, (), st.Bgf[0]))
                    if s + NSS < NSQ:
                        load_state(s + NSS, states[1 + (s + NSS) % NSS])
                o_evac(i)
                norm_part(i)

        _stage(3)
        pend = []

        def flush_pend():
            while pend:
                (sbk_, n_, q_, m_) = pend.pop(0)
                mm(pb[sbk_][:, 0:n_], onesb[:], xsq[q_][:, 0:n_], m_ == 0, m_ == 7, [Bones, Bxsq[q_]], [Bpb[sbk_]])

        def resid(bk, m, c0, n, gtoff):
            flush_pend()
            if c0 < NP:
                stt(xT[:, m, c0:c0 + n], pb[bk][:, 0:n], modT[:, gtoff + m, 0:1], xT[:, m, c0:c0 + n], MUL, ADD, [Bpb[bk], Bmod] + bx(m, c0, n), bx(m, c0, n))
            else:
                tm, Btm = tmp()
                tt("dve", v3(tm[:, 0:n]), v3(pb[bk][:, 0:n]), bc_seq(modT[:, gtoff + m, :]), MUL, [Bpb[bk], Bmod], [Btm])
                tt("pool", xT[:, m, c0:c0 + n], xT[:, m, c0:c0 + n], tm[:, 0:n], ADD, [Btm] + bx(m, c0, n), bx(m, c0, n))
            q = m % 2
            sbk = 6 if c0 < NP else 5
            act(xsq[q][:, 0:n], xT[:, m, c0:c0 + n], AF.Square, bx(m, c0, n), [Bxsq[q]])
            pend.append((sbk, n, q, m))

        def finish_stats():
            flush_pend()
            for (c0, n) in sbs:
                sbk = 6 if c0 < NP else 5
                act(rsb[:, c0:c0 + n], pb[sbk][:, 0:n], AF.Ln, [Bpb[sbk]], brs(c0, n), scale=1.0 / D, bias=EPS)
                act(rsb[:, c0:c0 + n], rsb[:, c0:c0 + n], AF.Exp, brs(c0, n), brs(c0, n), scale=-0.5)

        pbanks[:] = [0, 1, 2, 3, 4]
        for t_ in range(4):
            wt, Bw = wget()
            for ch in range(2):
                m = 2 * t_ + ch
                for (c0, n) in sbs:
                    bk = nbank()
                    for kc in range(8):
                        mm(pb[bk][:, 0:n], wt[:, kc, ch * 128:ch * 128 + 128], hT[:, kc, c0:c0 + n], kc == 0, kc == 7, [Bw] + hb(kc, c0, n), [Bpb[bk]])
                    resid(bk, m, c0, n, 16)

        def stats():
            for (c0, n) in sbs:
                for fc in range(8):
                    q = fc % 2
                    act(xsq[q][:, 0:n], xT[:, fc, c0:c0 + n], AF.Square, bx(fc, c0, n), [Bxsq[q]])
                    mm(pb[6][:, 0:n], onesb[:], xsq[q][:, 0:n], fc == 0, fc == 7, [Bones, Bxsq[q]], [Bpb[6]])
                act(rsb[:, c0:c0 + n], pb[6][:, 0:n], AF.Ln, [Bpb[6]], brs(c0, n), scale=1.0 / D, bias=EPS)
                act(rsb[:, c0:c0 + n], rsb[:, c0:c0 + n], AF.Exp, brs(c0, n), brs(c0, n), scale=-0.5)

        _stage(4)
        finish_stats()
        modulate(g2, 24)

        P.alias(Bhid, mixer_bufs)
        fb = [(0, 1), (2, 3), (4, 5)]
        fbi = 0
        for j in range(11):
            wa, Bwa = wget()
            wb_, Bwb = wget()
            for ch in range(2):
                k = 2 * j + ch
                for (c0, n) in sbs:
                    ba, bb = fb[fbi % 3]
                    fbi += 1
                    for kc in range(8):
                        mm(pb[ba][:, 0:n], wa[:, kc, ch * 128:ch * 128 + 128], hT[:, kc, c0:c0 + n], kc == 0, kc == 7, [Bwa] + hb(kc, c0, n), [Bpb[ba]])
                    for kc in range(8):
                        mm(pb[bb][:, 0:n], wb_[:, kc, ch * 128:ch * 128 + 128], hT[:, kc, c0:c0 + n], kc == 0, kc == 7, [Bwb] + hb(kc, c0, n), [Bpb[bb]])
                    tm, Btm = tmp()
                    act(tm[:, 0:n], pb[ba][:, 0:n], AF.Silu, [Bpb[ba]], [Btm])
                    tt("dve", hid[:, k, c0:c0 + n], tm[:, 0:n], pb[bb][:, 0:n], MUL, [Btm, Bpb[bb]], [Bhid[k]])

        if bi + 1 < len(BLOCKS):
            x_dma(bi + 1, 0)
            x_dma(bi + 1, 1)
        _stage(5)
        for m in range(8):
            w0, Bw0 = wget()
            w1, Bw1 = wget()
            for (c0, n) in sbs:
                bk = nbank()
                for kc in range(NKF):
                    w_, Bw_ = (w0, Bw0) if kc < 11 else (w1, Bw1)
                    mm(pb[bk][:, 0:n], w_[:, kc % 11, :], hid[:, kc, c0:c0 + n], kc == 0, kc == NKF - 1, [Bw_, Bhid[kc]], [Bpb[bk]])
                resid(bk, m, c0, n, 40)

        _stage(6)
        finish_stats()
        nxt = bi + 1 if bi + 1 < len(BLOCKS) else None
        if nxt is not None:
            p1a_begin(nxt)
        nnx = len(BLOCKS[nxt]) if nxt is not None else 0
        def y_mul(i):
            c0 = 128 * i
            q = i % 2
            for fc in range(8):
                stt(yT[q][:, fc, :], xT[:, fc, c0:c0 + 128], vc[:, V_FINN + fc:V_FINN + fc + 1], rsb[:, c0:c0 + 128], MUL, MUL, bx(fc, c0, 128) + [Bvc] + brs(c0, 128), ByT[q])

        y_mul(0)
        for i in range(max(nch, nnx)):
            if i < nch:
                kind, idx = chs[i]
                q = i % 2
                bks = [(4, 5), (2, 3)][q]
                for fc in range(8):
                    bk = bks[fc // 4]
                    tr(pb[bk][:, (fc % 4) * 128:(fc % 4) * 128 + 128], yT[q][:, fc, :], ident, ByT[q] + [Bcst], [Bpb[bk]])
                if i + 1 < nch:
                    y_mul(i + 1)
                ta, Ba = tmp()
                tb_, Bb = tmp()
                cp("act", ta[:], pb[bks[0]][:], [Bpb[bks[0]]], [Ba])
                cp("dve", tb_[:], pb[bks[1]][:], [Bpb[bks[1]]], [Bb])
                dst = yp_d[128 * idx:128 * idx + 128, :] if kind == "p" else ys_d
                out_dmas.append(dma("sp", dst[:, 0:512], ta[:], [Ba], (), Ba))
                out_dmas.append(dma("sp", dst[:, 512:1024], tb_[:], [Bb], (), Bb))
            if i < nnx:
                p1a_chunk(nxt, i)
                if i + 2 < nnx:
                    x_dma(nxt, i + 2)

    try:
        _stage(0)
        for bi in range(len(BLOCKS)):
            run_block(bi)
    except _Stop:
        pass
    if KSTAGE < 99:
        dbg_h = dout("dbg_h", [128, 8, TBMAX])
        dbg_x = dout("dbg_x", [128, 8, TBMAX])
        Bd1, Bd2 = P.buf("dbg1"), P.buf("dbg2")
        dma("pool", dbg_h, hT[:], [b_ for r_ in Bh for b_ in r_], (), Bd1)
        dma("sp", dbg_x, xT[:], [b_ for r_ in BxT for b_ in r_], (), Bd2)

    P.op("sp", lambda h: h.nop(), extra=all_dmas)
    P.emit(es)
    es.close()
    return nc


_NC = None


def kernel(x_prompt, x_sample, state_ret, state_gla, c_prompt, c_sample,
           w_ada, b_ada, mix_norm, w_in, w_gk_up, b_gk_up, ret_norm, gla_norm,
           w_out, ffn_norm, w_gate_up, w_down, final_norm):
    global _NC
    f = lambda a: np.ascontiguousarray(np.asarray(a, dtype=np.float32))
    x_prompt, x_sample, state_ret, state_gla = f(x_prompt), f(x_sample), f(state_ret), f(state_gla)
    c_prompt, c_sample = f(c_prompt), f(c_sample)
    cst, cossw = host_consts()
    vecs = np.concatenate([f(mix_norm).reshape(8, 128), f(ffn_norm).reshape(8, 128), f(final_norm).reshape(8, 128),
                           f(ret_norm).reshape(4, 128), f(gla_norm).reshape(4, 128), f(b_gk_up).reshape(2, 128),
                           f(b_ada).reshape(48, 128)], axis=0)
    shared = {"w_ada": f(w_ada)[0], "w_in": f(w_in)[0], "w_gk": f(w_gk_up)[0], "w_out": f(w_out)[0],
              "w_gu": f(w_gate_up)[0], "w_dn": f(w_down)[0], "vecs": np.ascontiguousarray(vecs), "cst": cst, "cossw": cossw}
    in_maps = []
    for r in range(NCORES):
        m = dict(shared)
        m["xp"] = x_prompt[r]
        m["xs"] = np.ascontiguousarray(x_sample[NSQ * r:NSQ * (r + 1)].reshape(128, D))
        m["c17"] = np.ascontiguousarray(np.concatenate([c_prompt[r:r + 1], c_sample[NSQ * r:NSQ * (r + 1)]], axis=0))
        m["sret"] = np.ascontiguousarray(state_ret[0, NSQ * r:NSQ * (r + 1)].reshape(NSQ, 512, 128))
        m["sgla"] = np.ascontiguousarray(state_gla[0, NSQ * r:NSQ * (r + 1)].reshape(NSQ, 256, 128))
        in_maps.append(m)
    if _NC is None:
        _NC = build()
    res = run_bass_kernel_spmd(_NC, in_maps, core_ids=list(range(NCORES)))
    rs = res.results
    if KSTAGE < 99:
        return rs
    y_prompt = np.stack([rs[r]["yp"] for r in range(NCORES)], axis=0).astype(np.float32)
    y_sample = np.concatenate([rs[r]["ys"].reshape(NSQ, DSEQ, D) for r in range(NCORES)], axis=0).astype(np.float32)
    srp = np.stack([rs[r]["srp"].reshape(4, 128, 128) for r in range(NCORES)], axis=0)[None].astype(np.float32)
    sgp = np.stack([rs[r]["sgp"].reshape(4, 64, 128) for r in range(NCORES)], axis=0)[None].astype(np.float32)
    srs = np.concatenate([rs[r]["srs"].reshape(NSQ, 4, 128, 128) for r in range(NCORES)], axis=0)[None].astype(np.float32)
    sgs = np.concatenate([rs[r]["sgs"].reshape(NSQ, 4, 64, 128) for r in range(NCORES)], axis=0)[None].astype(np.float32)
    return (y_prompt, y_sample, srp, sgp, srs, sgs)
```

```python
import numpy as np
from contextlib import ExitStack
import concourse.bass as bass
import concourse.mybir as mybir
from concourse.bass_utils import run_bass_kernel_spmd

F32 = mybir.dt.float32
BF16 = mybir.dt.bfloat16
AF = mybir.ActivationFunctionType
ALU = mybir.AluOpType
MUL, ADD = ALU.mult, ALU.add

ENGS = ("pe", "act", "dve", "pool", "sp")
import os
KSTAGE = int(os.environ.get("KSTAGE", "99"))


class _Stop(Exception):
    pass


def _stage(n):
    if KSTAGE == n:
        raise _Stop()


class Buf:
    __slots__ = ("name", "w", "r", "rd", "sem", "cnt", "extra")

    def __init__(self, name):
        self.name = name
        self.w = None
        self.r = {}
        self.rd = []
        self.sem = None
        self.cnt = 0
        self.extra = []


class Op:
    __slots__ = ("eng", "fn", "deps", "sig", "dma", "dbuf", "val", "sidx")

    def __init__(self, eng, fn, dma, dbuf):
        self.eng = eng
        self.fn = fn
        self.deps = []
        self.sig = False
        self.dma = dma
        self.dbuf = dbuf
        self.val = 0
        self.sidx = 0


class Prog:
    def __init__(self, nc):
        self.nc = nc
        self.ops = {e: [] for e in ENGS}
        self.all = []
        self.nbuf = 0

    def buf(self, name=None):
        self.nbuf += 1
        return Buf(name or f"b{self.nbuf}")

    def bufs(self, n, name="b"):
        return [self.buf(f"{name}{i}") for i in range(n)]

    def alias(self, dst, src):
        ops = []
        for s in src:
            if s.w is not None:
                ops.append(s.w)
            ops.extend(s.r.values())
            ops.extend(s.rd)
            ops.extend(s.extra)
        for d in dst:
            d.extra.extend(ops)

    def op(self, eng, fn, reads=(), writes=(), dma=False, dbuf=None, extra=()):
        o = Op(eng, fn, dma, dbuf)
        if dma:
            qt = "sw" if eng == "pool" else "hw"
            if dbuf.sem is None:
                dbuf.sem = {}
            ent = dbuf.sem.setdefault(qt, [None, 0])
            ent[1] += 16
            o.val = ent[1]
            o.sidx = qt
            o.sig = True
        deps = {}

        def add(d, raw):
            if d is None:
                return
            if (not d.dma) and (not dma) and d.eng == eng:
                if eng == "pe" or not raw:
                    return
            deps[id(d)] = d

        for b in reads:
            add(b.w, True)
        for b in writes:
            add(b.w, False)
            for r in b.r.values():
                add(r, False)
            for r in b.rd:
                add(r, False)
            for r in b.extra:
                add(r, False)
        for d in extra:
            deps[id(d)] = d
        o.deps = list(deps.values())
        for d in o.deps:
            d.sig = True
        for b in writes:
            b.w = o
            b.r = {}
            b.rd = []
            b.extra = []
        for b in reads:
            if dma:
                b.rd.append(o)
            else:
                b.r[eng] = o
        self.ops[eng].append(o)
        self.all.append(o)
        return o

    def emit(self, es):
        nc = self.nc
        esem = {e: es.enter_context(nc.semaphore(f"s_{e}")) for e in ENGS}
        nsem = len(ENGS)
        for o in self.all:
            if o.dma and o.dbuf.sem[o.sidx][0] is None:
                o.dbuf.sem[o.sidx][0] = es.enter_context(nc.semaphore(f"d{o.sidx}_{o.dbuf.name}"))
                nsem += 1
        self.nsem = nsem
        for e in ENGS:
            c = 0
            for o in self.ops[e]:
                if o.sig and not o.dma:
                    c += 1
                    o.sidx = c
        block = es.enter_context(nc.Block())

        def run(e, handle):
            seen = {}
            for o in self.ops[e]:
                ws = {}
                for d in o.deps:
                    if d.dma:
                        s, v = d.dbuf.sem[d.sidx][0], d.val
                    else:
                        s, v = esem[d.eng], d.sidx
                    k = id(s)
                    if seen.get(k, 0) >= v:
                        continue
                    if k not in ws or ws[k][1] < v:
                        ws[k] = (s, v)
                for k, (s, v) in ws.items():
                    handle.wait_ge(s, v)
                    seen[k] = v
                ins = o.fn(handle)
                if o.dma:
                    ins.then_inc(o.dbuf.sem[o.sidx][0], 16)
                elif o.sig:
                    ins.then_inc(esem[e], 1)

        @block.tensor
        def _(h):
            run("pe", h)

        @block.scalar
        def _(h):
            run("act", h)

        @block.vector
        def _(h):
            run("dve", h)

        @block.gpsimd
        def _(h):
            run("pool", h)

        @block.sync
        def _(h):
            run("sp", h)


D = 1024
NFC = 8
SEQ = 2048
NSQ = 16
DSEQ = 8
PAST = 16384
DFF = 2816
NKF = 22
INDIM = 3600
EPS = 1e-6
NCORES = int(os.environ.get('KCORES', '8'))
BLOCKS = [[("p", 0), ("p", 1), ("p", 2), ("p", 3), ("s", 0)],
          [("p", 4), ("p", 5), ("p", 6), ("p", 7)],
          [("p", 8), ("p", 9), ("p", 10), ("p", 11)],
          [("p", 12), ("p", 13), ("p", 14), ("p", 15)]]
TBMAX = 640
NTOK = SEQ + NSQ * DSEQ
GAM = [1.0 - 2.0 ** (-5 - h) for h in range(4)]

C_E1R, C_E2R, C_EHR, C_E1S, C_E2S, C_EHS = 0, 512, 1024, 1536, 2048, 2560
C_M01, C_MSM, C_MSEL, C_ID, C_SMP, C_SMS = 3072, 3200, 3328, 3344, 3472, 3600
NCR = 3728
V_MIXN, V_FFNN, V_FINN, V_RETN, V_GLAN, V_BGK, V_BADA, NV = 0, 8, 16, 24, 28, 32, 34, 82


def host_consts():
    half = 64
    inv = 10000.0 ** (-np.arange(half, dtype=np.float64) / half)
    pos_p = np.arange(SEQ, dtype=np.float64)
    pos_s = np.tile(PAST + np.arange(DSEQ, dtype=np.float64), NSQ)
    pos = np.concatenate([pos_p[:512], pos_s, pos_p[512:]])
    ang = pos[None, :] * np.concatenate([inv, inv])[:, None]
    cosT = np.cos(ang)
    sw = np.sin(ang)
    sw[64:] *= -1.0
    cossw = np.stack([cosT, sw], axis=1).astype(np.float32)
    cst = np.zeros((128, NCR), np.float64)
    t = np.arange(128, dtype=np.float64)
    ts = t % 8
    for h in range(4):
        lg = np.log1p(-2.0 ** (-5 - h))
        cst[:, C_E1R + 128 * h:C_E1R + 128 * h + 128] = np.exp(lg * (t + 1))[None]
        cst[:, C_E2R + 128 * h:C_E2R + 128 * h + 128] = (np.exp(-lg * (t + 1)) * 128 ** -0.5)[None]
        cst[:, C_EHR + 128 * h:C_EHR + 128 * h + 128] = (np.exp(lg * (127 - t)) * 128 ** -0.5)[None]
        cst[:, C_E1S + 128 * h:C_E1S + 128 * h + 128] = np.exp(lg * (ts + 1))[None]
        cst[:, C_E2S + 128 * h:C_E2S + 128 * h + 128] = (np.exp(-lg * (ts + 1)) * 128 ** -0.5)[None]
        cst[:, C_EHS + 128 * h:C_EHS + 128 * h + 128] = (np.exp(lg * (7 - ts)) * 128 ** -0.5)[None]
    j = np.arange(128)[:, None]
    i = np.arange(128)[None, :]
    cst[:, C_M01:C_M01 + 128] = (i >= j)
    cst[:, C_MSM:C_MSM + 128] = (i >= j) & ((i // 8) == (j // 8))
    cst[:, C_MSEL:C_MSEL + 16] = ((j // 8) == np.arange(16)[None, :])
    cst[:, C_ID:C_ID + 128] = (i == j)
    cst[:, C_SMP:C_SMP + 128] = (i != 0)
    cst[:, C_SMS:C_SMS + 128] = ((i % 8) != 0)
    return cst.astype(np.float32), cossw


def build():
    nc = bass.Bass("TRN2", target_bir_lowering=False)

    def din(name, shape):
        return nc.dram_tensor(name, list(shape), F32, kind="ExternalInput").ap()

    def dout(name, shape):
        return nc.dram_tensor(name, list(shape), F32, kind="ExternalOutput").ap()

    xp_d = din("xp", [SEQ, D])
    xs_d = din("xs", [128, D])
    c_d = din("c17", [17, D])
    sret_d = din("sret", [NSQ, 512, 128])
    sgla_d = din("sgla", [NSQ, 256, 128])
    wada_d = din("w_ada", [D, 6 * D])
    win_d = din("w_in", [D, INDIM])
    wgk_d = din("w_gk", [16, 256])
    wout_d = din("w_out", [D, D])
    wgu_d = din("w_gu", [D, 2 * DFF])
    wdn_d = din("w_dn", [DFF, D])
    vec_d = din("vecs", [NV, 128])
    cst_d = din("cst", [128, NCR])
    cs_d = din("cossw", [128, 2, NTOK])
    yp_d = dout("yp", [SEQ, D])
    ys_d = dout("ys", [128, D])
    srp_d = dout("srp", [512, 128])
    sgp_d = dout("sgp", [256, 128])
    srs_d = dout("srs", [NSQ, 512, 128])
    sgs_d = dout("sgs", [NSQ, 256, 128])

    es = ExitStack()
    P = Prog(nc)

    def sb(name, shape, dt=F32):
        return es.enter_context(nc.sbuf_tensor("sb_" + name, list(shape), dt))

    pb = [es.enter_context(nc.psum_tensor(f"pb{i}", [128, 512], F32)) for i in range(7)]
    pb7 = es.enter_context(nc.psum_tensor("pb7", [128, 1024], BF16))
    Bpb = P.bufs(8, "pb")

    def mm(out, lhsT, rhs, start, stop, reads, writes, sgc=False):
        return P.op("pe", lambda h: h.matmul(out, lhsT=lhsT, rhs=rhs, start=start, stop=stop, skip_group_check=sgc), reads, writes)

    def tr(out, in_, ident, reads, writes):
        return P.op("pe", lambda h: h.transpose(out=out, in_=in_, identity=ident), reads, writes)

    def act(out, in_, func, reads, writes, **kw):
        return P.op("act", lambda h: h.activation(out=out, in_=in_, func=func, **kw), reads, writes)

    def tt(eng, out, in0, in1, op, reads, writes):
        return P.op(eng, lambda h: h.tensor_tensor(out=out, in0=in0, in1=in1, op=op), reads, writes)

    def stt(out, in0, scalar, in1, op0, op1, reads, writes):
        return P.op("dve", lambda h: h.scalar_tensor_tensor(out=out, in0=in0, scalar=scalar, in1=in1, op0=op0, op1=op1), reads, writes)

    def tsc(eng, out, in0, s1, s2, op0, op1, reads, writes):
        if s2 is None and eng == "pool":
            s2, op1 = 0.0, ADD
        if s2 is None:
            return P.op(eng, lambda h: h.tensor_scalar(out=out, in0=in0, scalar1=s1, scalar2=None, op0=op0), reads, writes)
        return P.op(eng, lambda h: h.tensor_scalar(out=out, in0=in0, scalar1=s1, scalar2=s2, op0=op0, op1=op1), reads, writes)

    def cp(eng, out, in_, reads, writes):
        if eng == "act":
            return act(out, in_, AF.Copy, reads, writes)
        return P.op(eng, lambda h: h.tensor_copy(out=out, in_=in_), reads, writes)

    all_dmas = []

    def dma(q, out, in_, reads, writes, dbuf):
        o = P.op(q, lambda h: h.dma_start(out=out, in_=in_), reads, writes, dma=True, dbuf=dbuf)
        all_dmas.append(o)
        return o

    def mset(eng, ap, val, writes):
        return P.op(eng, lambda h: h.memset(ap, val), (), writes)

    out_dmas = []

    cst = sb("cst", [128, NCR])
    Bcst = P.buf("cst")
    identb = sb("identb", [128, 128], BF16)
    onesb = sb("onesb", [128, 128], BF16)
    onesf = sb("onesf", [128, 128])
    Bid = P.buf("idb")
    Bones = P.buf("ones")
    vin = sb("vin", [NV, 128])
    vc = sb("vc", [128, NV])
    nbgk = sb("nbgk", [128, 2])
    Bvin, Bvc = P.buf("vin"), P.buf("vc")
    cT = sb("cT", [128, 8, 17], BF16)
    BcT = P.buf("cT")
    modT = sb("modT", [128, 48, 17])
    g1 = sb("g1", [128, 8, 17])
    g2 = sb("g2", [128, 8, 17])
    Bmod = P.buf("mod")
    wgk = sb("wgk", [16, 256], BF16)
    Bwgk = P.buf("wgk")
    smask = sb("smask", [128, TBMAX])
    Bsmask = P.buf("smask")

    ident = cst[:, C_ID:C_ID + 128]

    NSLOT = 5
    wslots = [sb(f"wslot{i}", [128, 2048], BF16) for i in range(NSLOT)]
    Bws = P.bufs(NSLOT, "ws")

    xT = sb("xT", [128, 8, TBMAX])
    BxT = [[P.buf(f"xT{fc}_{i}") for i in range(5)] for fc in range(8)]

    def bx(fc, c0, n):
        return BxT[fc][c0 // 128:(c0 + n) // 128]
    hT = sb("hT", [128, 8, TBMAX], BF16)
    Bh = [[P.buf(f"h{fc}_{i}") for i in range(5)] for fc in range(8)]
    cossw = sb("cossw", [128, 2, TBMAX])
    Bcs = P.buf("cossw")
    rsb = sb("rsb", [128, TBMAX])
    Brsbs = P.bufs(5, "rsb")

    def brs(c0, n):
        return Brsbs[c0 // 128:(c0 + n) // 128]
    A_QT, A_KT, A_KH = 0, 4 * TBMAX, 8 * TBMAX
    A_V = 12 * TBMAX
    A_SG = A_V + 5 * 1024
    A_QTG = A_SG + 8 * TBMAX
    A_KTAB = A_QTG + 2 * TBMAX
    A_KHG = A_KTAB + 4 * TBMAX
    A_END = A_KHG + 2 * TBMAX
    assert A_END >= NKF * TBMAX
    arena = sb("arena", [128, A_END], BF16)

    def aview(off, n, w):
        return arena[:, off:off + n * w].rearrange("p (a b) -> p a b", b=w)

    qt_r, kt_r, kh_r = aview(A_QT, 4, TBMAX), aview(A_KT, 4, TBMAX), aview(A_KH, 4, TBMAX)
    v_tok = aview(A_V, 5, 1024)
    sg = aview(A_SG, 8, TBMAX)
    qt_g, ktAB, kh_g = aview(A_QTG, 2, TBMAX), aview(A_KTAB, 4, TBMAX), aview(A_KHG, 2, TBMAX)
    hid = aview(0, NKF, TBMAX)
    Bqt, Bkt, Bkh = P.bufs(4, "qt"), P.bufs(4, "kt"), P.bufs(4, "kh")
    Bv = P.bufs(5, "v")
    Bsg = P.bufs(8, "sg")
    Bqtg, Bktab, Bkhg = P.bufs(2, "qtg"), P.bufs(2, "ktab"), P.bufs(2, "khg")
    Bhid = P.bufs(NKF, "hid")
    mixer_bufs = Bqt + Bkt + Bkh + Bv + Bsg + Bqtg + Bktab + Bkhg

    lrT = sb("lrT", [16, TBMAX], BF16)
    Blr = P.buf("lr")
    spb = sb("spb", [128, 2, TBMAX])
    bsum = sb("bsum", [128, 2, TBMAX])
    E1, E2 = bsum, spb
    Bspb, Bbsum = P.bufs(2, "spb"), P.bufs(2, "bsum")
    BE1, BE2 = Bbsum, Bspb
    NTMP = 6
    tmps = [sb(f"tmp{i}", [128, 512]) for i in range(NTMP)]
    Btmp = P.bufs(NTMP, "tmp")
    tmp_i = [0]

    def tmp():
        k = tmp_i[0] % NTMP
        tmp_i[0] += 1
        return tmps[k], Btmp[k]

    xin = [sb(f"xin{i}", [128, D]) for i in range(2)]
    Bxin = P.bufs(2, "xin")
    cin, Bcin = xin[1], Bxin[1]
    ssq = sb("ssq", [128, 2])
    diag = [sb(f"diag{i}", [128, 128]) for i in range(2)]
    Bssq, Bdiag = P.bufs(2, "ssq"), P.bufs(2, "diag")

    attm = [sb(f"attm{i}", [128, 1024], BF16) for i in range(2)]
    Battm = P.bufs(2, "attm")
    osq = [sb(f"osq{i}", [128, 1024], BF16) for i in range(2)]
    Bosq = [P.bufs(2, f"osq{i}") for i in range(2)]
    rinv = sb("rinv", [128, 1024])
    Brinv = P.bufs(2, "rinv")
    osb = [sb(f"osb{i}", [128, 1024]) for i in range(2)]
    Bosb = [P.bufs(2, f"osb{i}") for i in range(2)]
    sqj, Bsqj = osq[0], Bosq[0][0]
    xsq = [osq[0][:, 0:512], osq[1][:, 0:512]]
    Bxsq = [Bosq[0][0], Bosq[1][0]]
    yT = [osb[0][:].rearrange("p (a b) -> p a b", b=128), osb[1][:].rearrange("p (a b) -> p a b", b=128)]
    ByT = [Bosb[0], Bosb[1]]
    NKHT = 2
    khT = [sb(f"khT{i}", [128, 768], BF16) for i in range(NKHT)]
    BkhT = P.bufs(NKHT, "khT")
    khT_i = [0]

    NSS = 4

    class State:
        pass

    states = []
    for k in range(1 + NSS):
        st = State()
        st.rf = sb(f"Srf{k}", [128, 4, 128])
        st.gf = sb(f"Sgf{k}", [128, 2, 128])
        st.rb = sb(f"Srb{k}", [128, 4, 128], BF16)
        st.gb = sb(f"Sgb{k}", [128, 4, 128], BF16)
        st.Brf, st.Bgf, st.Brb, st.Bgb = P.bufs(4, f"Srf{k}_"), P.bufs(4, f"Sgf{k}_"), P.buf(f"Srb{k}"), P.buf(f"Sgb{k}")
        states.append(st)

    jobs = []
    NJB = 15 + 4 + 22 + 16
    wscr = nc.dram_tensor("wscr", [NJB, 128, 2048], BF16, kind="Internal").ap()
    Bscr = P.bufs(NJB, "wscr")

    def job_cols(w, c0, ncols, cidx=None, fresh=True):
        src = w[:, c0:c0 + ncols].rearrange("(kc k) c -> k kc c", k=128) if fresh else None
        return (lambda s: s[:, 0:8 * ncols].rearrange("p (k c) -> p k c", c=ncols), src, cidx, 8 * ncols)

    def job_down(m, half, cidx=None, fresh=True):
        src = wdn_d[half * 1408:(half + 1) * 1408, 128 * m:128 * m + 128].rearrange("(kc k) c -> k kc c", k=128) if fresh else None
        return (lambda s: s[:, 0:11 * 128].rearrange("p (k c) -> p k c", c=128), src, cidx, 11 * 128)

    for t_ in range(8):
        jobs.append(job_cols(wada_d, 256 * t_, 256))
    WIN_ORDER = [("lr", 3584, 16), ("v", 1024, 256), ("v", 1280, 256), ("qg", 2048, 256), ("kg", 2304, 256),
                 ("qr", 0, 256), ("gr", 1536, 256), ("qr", 256, 256), ("gr", 1792, 256),
                 ("kr", 512, 256), ("gg", 3072, 256), ("kr", 768, 256), ("gg", 3328, 256),
                 ("v", 2560, 256), ("v", 2816, 256)]
    for b_ in range(len(BLOCKS)):
        fr = b_ == 0
        ci = 0
        for k_, (_, c0, n_) in enumerate(WIN_ORDER):
            jobs.append(job_cols(win_d, c0, n_, ci, fr))
            ci += 1
            if b_ == 0:
                jobs.append(job_cols(wada_d, 256 * (8 + k_), 256))
        if b_ == 0:
            jobs.append(job_cols(wada_d, 256 * 23, 256))
        for t_ in range(4):
            jobs.append(job_cols(wout_d, 256 * t_, 256, ci, fr))
            ci += 1
        for j_ in range(11):
            jobs.append(job_cols(wgu_d, 256 * j_, 256, ci, fr))
            jobs.append(job_cols(wgu_d, DFF + 256 * j_, 256, ci + 1, fr))
            ci += 2
        for m_ in range(8):
            jobs.append(job_down(m_, 0, ci, fr))
            jobs.append(job_down(m_, 1, ci + 1, fr))
            ci += 2
        assert ci == NJB
    wstate = {"issued": 0, "next": 0}

    def wget():
        n = wstate["next"]
        wstate["next"] += 1
        while wstate["issued"] < min(n + NSLOT - 1, len(jobs)):
            k = wstate["issued"]
            vf, src, cidx, L = jobs[k]
            s = k % NSLOT
            if src is not None:
                dma("pool", vf(wslots[s]), src, (), [Bws[s]], Bws[s])
                if cidx is not None:
                    dma("sp", wscr[cidx][:, 0:L], wslots[s][:, 0:L], [Bws[s]], [Bscr[cidx]], Bws[s])
            else:
                dma("sp", wslots[s][:, 0:L], wscr[cidx][:, 0:L], [Bscr[cidx]], [Bws[s]], Bws[s])
            wstate["issued"] += 1
        vf = jobs[n][0]
        return vf(wslots[n % NSLOT]), Bws[n % NSLOT]

    dma("sp", cst[:], cst_d, (), [Bcst], Bcst)
    dma("sp", vin[:], vec_d, (), [Bvin], Bvin)
    dma("sp", cin[0:17, :], c_d, (), [Bcin], Bcin)
    dma("pool", wgk[:], wgk_d, (), [Bwgk], Bwgk)
    cp("dve", identb[:], ident, [Bcst], [Bid])
    mset("pool", onesb[:], 1.0, [Bones])
    mset("pool", onesf[:], 1.0, [Bones])
    mset("pool", arena[:, A_KTAB:A_KTAB + 4 * TBMAX], 0.0, Bktab)
    for st in states:
        mset("pool", st.gb[:], 0.0, [st.Bgb])
    mset("pool", states[0].rf[:], 0.0, states[0].Brf)
    mset("pool", states[0].gf[:], 0.0, states[0].Bgf)
    cp("pool", smask[:, 0:512].rearrange("p (c t) -> p c t", t=128),
       cst[:, C_SMP:C_SMP + 128].unsqueeze(1).broadcast_to([128, 4, 128]), [Bcst], [Bsmask])
    cp("pool", smask[:, 512:640], cst[:, C_SMS:C_SMS + 128], [Bcst], [Bsmask])
    tr(pb[2][:, 0:NV], vin[0:NV, :], cst[0:NV, C_ID:C_ID + NV], [Bvin, Bcst], [Bpb[2]])
    cp("dve", vc[:], pb[2][:, 0:NV], [Bpb[2]], [Bvc])
    tsc("dve", nbgk[:], vc[:, V_BGK:V_BGK + 2], -1.0, None, MUL, None, [Bvc], [Bvc])
    for kc in range(8):
        tr(pb[3][:, kc * 17:(kc + 1) * 17], cin[0:17, kc * 128:(kc + 1) * 128], cst[0:17, C_ID:C_ID + 17], [Bcin, Bcst], [Bpb[3]])
    act(cT[:], pb[3][:, 0:136].rearrange("p (k s) -> p k s", s=17), AF.Silu, [Bpb[3]], [BcT])
    xloaded = set()

    def x_dma(bi, i):
        if (bi, i) in xloaded:
            return
        xloaded.add((bi, i))
        kind, idx = BLOCKS[bi][i]
        sl = i % 2
        src = xp_d[128 * idx:128 * idx + 128, :] if kind == "p" else xs_d
        dma("sp", xin[sl][:], src, (), [Bxin[sl]], Bxin[sl])

    def p1a_begin(bi):
        TB = 128 * len(BLOCKS[bi])
        col0 = 0 if bi == 0 else 640 + 512 * (bi - 1)
        dma("sp", cossw[:, :, 0:TB], cs_d[:, :, col0:col0 + TB], (), [Bcs], Bcs)

    def p1a_chunk(bi, i):
        c0 = 128 * i
        sl = i % 2
        x_dma(bi, i)
        act(sqj[:], xin[sl][:], AF.Square, [Bxin[sl]], [Bsqj, Bosq[0][1], Bssq[sl]], accum_out=ssq[:, sl:sl + 1])
        tsc("dve", diag[sl][:], ident, ssq[:, sl:sl + 1], None, MUL, None, [Bcst, Bssq[sl]], [Bdiag[sl]])
        for fc in range(8):
            bk = fc // 4
            tr(pb[bk][:, (fc % 4) * 128:(fc % 4) * 128 + 128], xin[sl][:, fc * 128:fc * 128 + 128], ident, [Bxin[sl], Bcst], [Bpb[bk]])
        mm(pb[6][:, 0:128], onesf[:], diag[sl][:], True, True, [Bones, Bdiag[sl]], [Bpb[6]])
        cp("act", xT[:, 0:4, c0:c0 + 128], pb[0][:].rearrange("p (a b) -> p a b", b=128), [Bpb[0]], [BxT[fc][i] for fc in range(4)])
        cp("dve", xT[:, 4:8, c0:c0 + 128], pb[1][:].rearrange("p (a b) -> p a b", b=128), [Bpb[1]], [BxT[fc][i] for fc in range(4, 8)])
        act(rsb[:, c0:c0 + 128], pb[6][:, 0:128], AF.Ln, [Bpb[6]], brs(c0, 128), scale=1.0 / D, bias=EPS)
        act(rsb[:, c0:c0 + 128], rsb[:, c0:c0 + 128], AF.Exp, brs(c0, 128), brs(c0, 128), scale=-0.5)

    p1a_begin(0)
    x_dma(0, 0)
    x_dma(0, 1)
    for i_ in range(len(BLOCKS[0])):
        p1a_chunk(0, i_)
        if i_ + 2 < len(BLOCKS[0]):
            x_dma(0, i_ + 2)
    def ada_tile(t_):
        wt, Bw = wget()
        bank = 4 + (t_ % 2)
        for o2 in range(2):
            for kc in range(8):
                mm(pb[bank][:, o2 * 17:(o2 + 1) * 17], wt[:, kc, o2 * 128:(o2 + 1) * 128], cT[:, kc, :], kc == 0, kc == 7, [Bw, BcT], [Bpb[bank]])
        for o2 in range(2):
            oc = 2 * t_ + o2
            act(modT[:, oc, :], pb[bank][:, o2 * 17:(o2 + 1) * 17], AF.Identity, [Bpb[bank], Bvc], [Bmod], bias=vc[:, V_BADA + oc:V_BADA + oc + 1])

    for t_ in range(8):
        ada_tile(t_)
    for fc in range(8):
        tsc("dve", g1[:, fc, :], modT[:, 8 + fc, :], 1.0, vc[:, V_MIXN + fc:V_MIXN + fc + 1], ADD, MUL, [Bmod, Bvc], [Bmod])

    def ada_finish():
        for fc in range(8):
            tsc("dve", g2[:, fc, :], modT[:, 32 + fc, :], 1.0, vc[:, V_FFNN + fc:V_FFNN + fc + 1], ADD, MUL, [Bmod, Bvc], [Bmod])

    def bc_seq(ap2d):
        return ap2d[:, 1:17].unsqueeze(2).broadcast_to([128, NSQ, DSEQ])

    def v3(ap):
        return ap.rearrange("p (s t) -> p s t", t=DSEQ)

    def run_block(bi):
        chs = BLOCKS[bi]
        nch = len(chs)
        has_s = chs[-1][0] == "s"
        npr = nch - (1 if has_s else 0)
        NP = 128 * npr
        TB = 128 * nch
        sbs = [(0, NP)] + ([(NP, 128)] if has_s else [])
        col0 = 0 if bi == 0 else 640 + 512 * (bi - 1)

        def hb(kc, c0, n):
            return Bh[kc][c0 // 128:(c0 + n) // 128]


        def modulate(gX, shoff):
            for fc in range(8):
                tm, Btm = tmp()
                stt(tm[:, 0:NP], xT[:, fc, 0:NP], gX[:, fc, 0:1], rsb[:, 0:NP], MUL, MUL, bx(fc, 0, NP) + [Bmod] + brs(0, NP), [Btm])
                act(hT[:, fc, 0:NP], tm[:, 0:NP], AF.Identity, [Btm, Bmod], hb(fc, 0, NP), bias=modT[:, shoff + fc, 0:1])
                if has_s:
                    t2, Bt2 = tmp()
                    tt("dve", t2[:, 0:128], xT[:, fc, NP:TB], rsb[:, NP:TB], MUL, bx(fc, NP, 128) + brs(NP, 128), [Bt2])
                    tt("pool", v3(t2[:, 0:128]), v3(t2[:, 0:128]), bc_seq(gX[:, fc, :]), MUL, [Bt2, Bmod], [Bt2])
                    tt("pool", v3(hT[:, fc, NP:TB]), v3(t2[:, 0:128]), bc_seq(modT[:, shoff + fc, :]), ADD, [Bt2, Bmod], hb(fc, NP, 128))

        modulate(g1, 0)
        _stage(1)

        if bi > 0:
            P.alias(mixer_bufs, Bhid)
        pbanks = [0, 1, 2, 3, 4, 5]
        pbi = [0]

        def nbank():
            k = pbanks[pbi[0] % len(pbanks)]
            pbi[0] += 1
            return k

        def proj(wt, Bw, ch, evac):
            for (c0, n) in sbs:
                bk = nbank()
                for kc in range(8):
                    mm(pb[bk][:, 0:n], wt[:, kc, ch * 128:ch * 128 + 128], hT[:, kc, c0:c0 + n], kc == 0, kc == 7, [Bw] + hb(kc, c0, n), [Bpb[bk]])
                evac(bk, c0, n)

        def etab(base, hh, c0, n):
            if c0 >= NP:
                return cst[:, base + 1536 + 128 * hh:base + 1536 + 128 * hh + 128]
            return cst[:, base + 128 * hh:base + 128 * hh + 128].unsqueeze(1).broadcast_to([128, n // 128, 128])

        def ch3(ap, c0, n):
            if c0 >= NP:
                return ap
            return ap.rearrange("p (c t) -> p c t", t=128)

        def rotary(bk, c0, n, add_eng="pool"):
            t1, B1 = tmp()
            t2, B2 = tmp()
            ps = pb[bk]
            tt("dve", t1[:, 0:n], ps[:, 0:n], cossw[:, 0, c0:c0 + n], MUL, [Bpb[bk], Bcs], [B1])
            tt("dve", t2[64:128, 0:n], ps[0:64, 0:n], cossw[0:64, 1, c0:c0 + n], MUL, [Bpb[bk], Bcs], [B2])
            tt("dve", t2[0:64, 0:n], ps[64:128, 0:n], cossw[64:128, 1, c0:c0 + n], MUL, [Bpb[bk], Bcs], [B2])
            tt(add_eng, t1[:, 0:n], t1[:, 0:n], t2[:, 0:n], ADD, [B1, B2], [B1])
            return t1, B1

        def ev_qr(hh):
            def f(bk, c0, n):
                r, Br = rotary(bk, c0, n)
                tt("pool", ch3(qt_r[:, hh, c0:c0 + n], c0, n), ch3(r[:, 0:n], c0, n), etab(C_E1R, hh, c0, n), MUL, [Br, Bcst], [Bqt[hh]])
            return f

        def ev_kr(hh):
            def f(bk, c0, n):
                r, Br = rotary(bk, c0, n, "dve")
                tt("pool", ch3(kt_r[:, hh, c0:c0 + n], c0, n), ch3(r[:, 0:n], c0, n), etab(C_E2R, hh, c0, n), MUL, [Br, Bcst], [Bkt[hh]])
                tt("pool", ch3(kh_r[:, hh, c0:c0 + n], c0, n), ch3(r[:, 0:n], c0, n), etab(C_EHR, hh, c0, n), MUL, [Br, Bcst], [Bkh[hh]])
            return f

        def ev_gate(hh):
            gcol = (V_RETN + hh) if hh < 4 else (V_GLAN + hh - 4)

            def f(bk, c0, n):
                tm, Btm = tmp()
                act(tm[:, 0:n], pb[bk][:, 0:n], AF.Silu, [Bpb[bk]], [Btm])
                act(sg[:, hh, c0:c0 + n], tm[:, 0:n], AF.Copy, [Btm, Bvc], [Bsg[hh]], scale=vc[:, gcol:gcol + 1])
            return f

        def ev_qg(p):
            def f(bk, c0, n):
                stt(qt_g[:, p, c0:c0 + n], pb[bk][:, 0:n], 0.125, E1[:, p, c0:c0 + n], MUL, MUL, [Bpb[bk], BE1[p]], [Bqtg[p]])
            return f

        def ev_kg(p):
            def f(bk, c0, n):
                tm, Btm = tmp()
                tt("dve", tm[:, 0:n], pb[bk][:, 0:n], E2[:, p, c0:c0 + n], MUL, [Bpb[bk], BE2[p]], [Btm])
                cp("act", ktAB[0:64, 2 * p, c0:c0 + n], tm[0:64, 0:n], [Btm], [Bktab[p]])
                cp("act", ktAB[64:128, 2 * p + 1, c0:c0 + n], tm[64:128, 0:n], [Btm], [Bktab[p]])
                if c0 >= NP:
                    e1l = v3(E1[:, p, c0:c0 + n])[:, :, DSEQ - 1:DSEQ].broadcast_to([128, NSQ, DSEQ])
                    tt("pool", v3(kh_g[:, p, c0:c0 + n]), v3(tm[:, 0:n]), e1l, MUL, [Btm, BE1[p]], [Bkhg[p]])
                else:
                    e1l = E1[:, p, c0:c0 + n].rearrange("p (c t) -> p c t", t=128)[:, :, 127:128].broadcast_to([128, n // 128, 128])
                    tt("pool", ch3(kh_g[:, p, c0:c0 + n], c0, n), ch3(tm[:, 0:n], c0, n), e1l, MUL, [Btm, BE1[p]], [Bkhg[p]])
            return f

        vt_i = [0]
        for wk_, (kind, wc0, wn) in enumerate(WIN_ORDER):
            if bi == 0 and wk_ > 0:
                ada_tile(8 + wk_ - 1)
            wt, Bw = wget()
            if kind == "lr":
                for (c0, n) in sbs:
                    bk = nbank()
                    for kc in range(8):
                        mm(pb[bk][0:16, 0:n], wt[:, kc, 0:16], hT[:, kc, c0:c0 + n], kc == 0, kc == 7, [Bw] + hb(kc, c0, n), [Bpb[bk]])
                    cp("act", lrT[0:16, c0:c0 + n], pb[bk][0:16, 0:n], [Bpb[bk]], [Blr])
                for p in range(2):
                    for (c0, n) in sbs:
                        bk = nbank()
                        mm(pb[bk][:, 0:n], wgk[0:16, 128 * p:128 * p + 128], lrT[0:16, c0:c0 + n], True, True, [Bwgk, Blr], [Bpb[bk]])
                        tm, Btm = tmp()
                        act(tm[:, 0:n], pb[bk][:, 0:n], AF.Exp, [Bpb[bk], Bvc], [Btm], scale=-1.0, bias=nbgk[:, p:p + 1])
                        act(spb[:, p, c0:c0 + n], tm[:, 0:n], AF.Ln, [Btm], [Bspb[p]], bias=1.0)
                    P.op("dve", lambda h, p=p: h.tensor_tensor_scan(out=bsum[:, p, 0:TB], data0=smask[:, 0:TB], data1=spb[:, p, 0:TB], initial=0.0, op0=MUL, op1=ADD),
                         [Bsmask, Bspb[p]], [Bbsum[p]])
                    act(E2[:, p, 0:TB], bsum[:, p, 0:TB], AF.Exp, [Bbsum[p]], [BE2[p]], scale=1.0 / 16.0)
                    act(E1[:, p, 0:TB], bsum[:, p, 0:TB], AF.Exp, [Bbsum[p]], [BE1[p]], scale=-1.0 / 16.0)
            elif kind == "v":
                vt = vt_i[0]
                vt_i[0] += 1
                for i in range(nch):
                    bk = nbank()
                    for kc in range(8):
                        mm(pb[bk][:, 0:256], hT[:, kc, 128 * i:128 * i + 128], wt[:, kc, 0:256], kc == 0, kc == 7, [Bw, Bh[kc][i]], [Bpb[bk]])
                    cp("act", v_tok[:, i, 256 * vt:256 * vt + 256], pb[bk][:, 0:256], [Bpb[bk]], [Bv[i]])
            else:
                for ch in range(2):
                    idx = (wc0 % 512) // 128 + ch if kind in ("qr", "kr", "gr", "gg") else ch
                    if kind == "qr":
                        proj(wt, Bw, ch, ev_qr(idx))
                    elif kind == "kr":
                        proj(wt, Bw, ch, ev_kr(idx))
                    elif kind == "gr":
                        proj(wt, Bw, ch, ev_gate(idx))
                    elif kind == "gg":
                        proj(wt, Bw, ch, ev_gate(4 + idx))
                    elif kind == "qg":
                        proj(wt, Bw, ch, ev_qg(ch))
                    elif kind == "kg":
                        proj(wt, Bw, ch, ev_kg(ch))

        if bi == 0:
            ada_tile(22)
            ada_tile(23)
            ada_finish()
        _stage(2)
        PB_ATT, PB_SU, PB_O = (2, 3), (0, 1), (4, 5)

        def k_part(i):
            c0 = 128 * i
            for hh in range(4):
                tr(pb7[:, 128 * hh:128 * hh + 128], kh_r[:, hh, c0:c0 + 128], identb[:], [Bkh[hh], Bid], [Bpb[7]])
            for p in range(2):
                tr(pb7[:, 512 + 128 * p:512 + 128 * p + 128], kh_g[:, p, c0:c0 + 128], identb[:], [Bkhg[p], Bid], [Bpb[7]])

        def khT_copy(i):
            kk = i % NKHT
            cp("act", khT[kk][:], pb7[:, 0:768], [Bpb[7]], [BkhT[kk]])

        def a1_part(i):
            c0 = 128 * i
            kind = chs[i][0]
            mask = cst[:, C_M01:C_M01 + 128] if kind == "p" else cst[:, C_MSM:C_MSM + 128]
            m4 = mask.unsqueeze(1).broadcast_to([128, 4, 128])
            am, Bam = attm[i % 2], Battm[i % 2]
            for hh in range(4):
                mm(pb[PB_ATT[0]][:, 128 * hh:128 * hh + 128], kt_r[:, hh, c0:c0 + 128], qt_r[:, hh, c0:c0 + 128], True, True, [Bkt[hh], Bqt[hh]], [Bpb[PB_ATT[0]]])
            for g in range(4):
                p = g // 2
                mm(pb[PB_ATT[1]][:, 128 * g:128 * g + 128], ktAB[:, g, c0:c0 + 128], qt_g[:, p, c0:c0 + 128], True, True, [Bktab[p], Bqtg[p]], [Bpb[PB_ATT[1]]])
            tt("dve", am[:, 0:512].rearrange("p (a b) -> p a b", b=128), pb[PB_ATT[0]][:].rearrange("p (a b) -> p a b", b=128), m4, MUL, [Bpb[PB_ATT[0]], Bcst], [Bam])
            tt("dve", am[:, 512:1024].rearrange("p (a b) -> p a b", b=128), pb[PB_ATT[1]][:].rearrange("p (a b) -> p a b", b=128), m4, MUL, [Bpb[PB_ATT[1]], Bcst], [Bam])

        def su_mm(i, kt_, Bk_, sub):
            for hh in range(4):
                mm(pb[sub[0]][:, 128 * hh:128 * hh + 128], kt_[:, 128 * hh:128 * hh + 128], v_tok[:, i, 128 * hh:128 * hh + 128], True, True, [Bk_, Bv[i]], [Bpb[sub[0]]])
            for p in range(2):
                mm(pb[sub[1]][:, 256 * p:256 * p + 256], kt_[:, 512 + 128 * p:512 + 128 * p + 128], v_tok[:, i, 512 + 256 * p:512 + 256 * p + 256], True, True, [Bk_, Bv[i]], [Bpb[sub[1]]])

        def s_upd(st, sub, dec, ecol):
            for hh in range(4):
                stt(st.rf[:, hh, :], st.rf[:, hh, :], float(dec[hh]), pb[sub[0]][:, 128 * hh:128 * hh + 128], MUL, ADD, [st.Brf[hh], Bpb[sub[0]]], [st.Brf[hh]])
            for p in range(2):
                for e in range(2):
                    r0 = 64 * e
                    stt(st.gf[r0:r0 + 64, p, :], st.gf[r0:r0 + 64, p, :], E1[r0:r0 + 64, p, ecol:ecol + 1],
                        pb[sub[1]][r0:r0 + 64, 256 * p + 128 * e:256 * p + 128 * e + 128], MUL, ADD, [st.Bgf[2 * p + e], BE1[p], Bpb[sub[1]]], [st.Bgf[2 * p + e]])

        def s_cast(st, eng="act"):
            cp(eng, st.rb[:], st.rf[:], st.Brf, [st.Brb])
            gbv = st.gb[:].rearrange("p (a e) v -> p a e v", e=2)
            cp("act", gbv[0:64, :, 0, :], st.gf[0:64, :, :], st.Bgf, [st.Bgb])
            cp("act", gbv[64:128, :, 1, :], st.gf[64:128, :, :], st.Bgf, [st.Bgb])

        def o_intra(i):
            am, Bam = attm[i % 2], Battm[i % 2]
            for hh in range(8):
                bk = PB_O[hh // 4]
                oc = 128 * (hh % 4)
                mm(pb[bk][:, oc:oc + 128], v_tok[:, i, 128 * hh:128 * hh + 128], am[:, 128 * hh:128 * hh + 128], hh % 4 == 0, True, [Bv[i], Bam], [Bpb[bk]], sgc=True)

        def inter(i, s0, sn, st):
            c0 = 128 * i
            for hh in range(8):
                bk = PB_O[hh // 4]
                oc = 128 * (hh % 4)
                if hh < 4:
                    mm(pb[bk][:, oc + s0:oc + s0 + sn], st.rb[:, hh, :], qt_r[:, hh, c0 + s0:c0 + s0 + sn], False, True, [st.Brb, Bqt[hh]], [Bpb[bk]], sgc=True)
                else:
                    g = hh - 4
                    mm(pb[bk][:, oc + s0:oc + s0 + sn], st.gb[:, g, :], qt_g[:, g // 2, c0 + s0:c0 + s0 + sn], False, True, [st.Bgb, Bqtg[g // 2]], [Bpb[bk]], sgc=True)

        def o_evac(i):
            q = i % 2
            for grp in range(2):
                sl = slice(512 * grp, 512 * grp + 512)
                act(osq[q][:, sl], pb[PB_O[grp]][:], AF.Square, [Bpb[PB_O[grp]]], [Bosq[q][grp]])
                cp("dve", osb[q][:, sl], pb[PB_O[grp]][:], [Bpb[PB_O[grp]]], [Bosb[q][grp]])

        def norm_part(i):
            c0 = 128 * i
            q = i % 2
            for grp in range(2):
                sl = slice(512 * grp, 512 * grp + 512)
                r3 = rinv[:, sl].rearrange("p (a b) -> p a b", b=128)
                mm(pb[6][:], onesb[:], osq[q][:, sl], True, True, [Bones, Bosq[q][grp]], [Bpb[6]])
                act(rinv[:, sl], pb[6][:], AF.Ln, [Bpb[6]], [Brinv[grp]], scale=1.0 / 128.0, bias=EPS)
                act(rinv[:, sl], rinv[:, sl], AF.Exp, [Brinv[grp]], [Brinv[grp]], scale=-0.5)
                tt("pool", r3, r3, sg[:, 4 * grp:4 * grp + 4, c0:c0 + 128], MUL, [Brinv[grp]] + Bsg[4 * grp:4 * grp + 4], [Brinv[grp]])
                tt("pool", hT[:, 4 * grp:4 * grp + 4, c0:c0 + 128], osb[q][:, sl].rearrange("p (a b) -> p a b", b=128), r3, MUL, [Bosb[q][grp], Brinv[grp]], [Bh[fc_][i] for fc_ in range(4 * grp, 4 * grp + 4)])

        k_part(0)
        if chs[0][0] == "p":
            khT_copy(0)
        a1_part(0)
        for i, (kind, idx) in enumerate(chs):
            c0 = 128 * i
            if kind == "p":
                st = states[0]
                kk = i % NKHT
                su_mm(i, khT[kk], BkhT[kk], PB_SU)
                o_intra(i)
                if idx > 0:
                    inter(i, 0, 128, st)
                s_upd(st, PB_SU, [g_ ** 128 for g_ in GAM], c0 + 127)
                o_evac(i)
                if i + 1 < nch:
                    k_part(i + 1)
                    if chs[i + 1][0] == "p":
                        khT_copy(i + 1)
                    a1_part(i + 1)
                if idx < 15:
                    s_cast(st, "dve")
                else:
                    out_dmas.append(dma("sp", srp_d.rearrange("(h d) v -> d h v", d=128), st.rf[:], st.Brf, (), st.Brf[0]))
                    out_dmas.append(dma("sp", sgp_d.rearrange("(p q) v -> q p v", q=128), st.gf[:], st.Bgf, (), st.Bgf[0]))
                norm_part(i)
            else:
                loaded = {}

                def load_state(s, st):
                    if s in loaded:
                        return
                    loaded[s] = True
                    dma("pool", st.rf[:], sret_d[s].rearrange("(h d) v -> d h v", d=128), (), st.Brf, st.Brf[0])
                    dma("pool", st.gf[:], sgla_d[s].rearrange("(p q) v -> q p v", q=128), (), st.Bgf, st.Bgf[0])

                for s in range(NSS):
                    load_state(s, states[1 + s])
                o_intra(i)
                dec8 = [g_ ** 8 for g_ in GAM]
                for s in range(NSQ):
                    st = states[1 + s % NSS]
                    ecol = c0 + DSEQ * s + DSEQ - 1
                    s_cast(st, "act")
                    inter(i, DSEQ * s, DSEQ, st)
                    kk = khT_i[0] % NKHT
                    khT_i[0] += 1
                    kt_, Bk_ = khT[kk], BkhT[kk]
                    act(kt_[:], pb7[:, 0:768], AF.Copy, [Bpb[7], Bcst], [Bk_], scale=cst[:, C_MSEL + s:C_MSEL + s + 1])
                    sub = [PB_SU, PB_ATT][s % 2]
                    su_mm(i, kt_, Bk_, sub)
                    s_upd(st, sub, dec8, ecol)
                    out_dmas.append(dma("sp", srs_d[s].rearrange("(h d) v -> d h v", d=128), st.rf[:], st.Brf, (), st.Brf[0]))
                    out_dmas.append(dma("sp", sgs_d[s].rearrange("(p q) v -> q p v", q=128), st.gf[:], st.Bgf, (), st.Bgf[0]))
                    if s + NSS < NSQ:
                        load_state(s + NSS, states[1 + (s + NSS) % NSS])
                o_evac(i)
                norm_part(i)

        _stage(3)
        pend = []

        def flush_pend():
            while pend:
                (sbk_, n_, q_, m_) = pend.pop(0)
                mm(pb[sbk_][:, 0:n_], onesb[:], xsq[q_][:, 0:n_], m_ == 0, m_ == 7, [Bones, Bxsq[q_]], [Bpb[sbk_]])

        def resid(bk, m, c0, n, gtoff):
            flush_pend()
            if c0 < NP:
                stt(xT[:, m, c0:c0 + n], pb[bk][:, 0:n], modT[:, gtoff + m, 0:1], xT[:, m, c0:c0 + n], MUL, ADD, [Bpb[bk], Bmod] + bx(m, c0, n), bx(m, c0, n))
            else:
                tm, Btm = tmp()
                tt("dve", v3(tm[:, 0:n]), v3(pb[bk][:, 0:n]), bc_seq(modT[:, gtoff + m, :]), MUL, [Bpb[bk], Bmod], [Btm])
                tt("pool", xT[:, m, c0:c0 + n], xT[:, m, c0:c0 + n], tm[:, 0:n], ADD, [Btm] + bx(m, c0, n), bx(m, c0, n))
            q = m % 2
            sbk = 6 if c0 < NP else 5
            act(xsq[q][:, 0:n], xT[:, m, c0:c0 + n], AF.Square, bx(m, c0, n), [Bxsq[q]])
            pend.append((sbk, n, q, m))

        def finish_stats():
            flush_pend()
            for (c0, n) in sbs:
                sbk = 6 if c0 < NP else 5
                act(rsb[:, c0:c0 + n], pb[sbk][:, 0:n], AF.Ln, [Bpb[sbk]], brs(c0, n), scale=1.0 / D, bias=EPS)
                act(rsb[:, c0:c0 + n], rsb[:, c0:c0 + n], AF.Exp, brs(c0, n), brs(c0, n), scale=-0.5)

        pbanks[:] = [0, 1, 2, 3, 4]
        for t_ in range(4):
            wt, Bw = wget()
            for ch in range(2):
                m = 2 * t_ + ch
                for (c0, n) in sbs:
                    bk = nbank()
                    for kc in range(8):
                        mm(pb[bk][:, 0:n], wt[:, kc, ch * 128:ch * 128 + 128], hT[:, kc, c0:c0 + n], kc == 0, kc == 7, [Bw] + hb(kc, c0, n), [Bpb[bk]])
                    resid(bk, m, c0, n, 16)

        def stats():
            for (c0, n) in sbs:
                for fc in range(8):
                    q = fc % 2
                    act(xsq[q][:, 0:n], xT[:, fc, c0:c0 + n], AF.Square, bx(fc, c0, n), [Bxsq[q]])
                    mm(pb[6][:, 0:n], onesb[:], xsq[q][:, 0:n], fc == 0, fc == 7, [Bones, Bxsq[q]], [Bpb[6]])
                act(rsb[:, c0:c0 + n], pb[6][:, 0:n], AF.Ln, [Bpb[6]], brs(c0, n), scale=1.0 / D, bias=EPS)
                act(rsb[:, c0:c0 + n], rsb[:, c0:c0 + n], AF.Exp, brs(c0, n), brs(c0, n), scale=-0.5)

        _stage(4)
        finish_stats()
        modulate(g2, 24)

        P.alias(Bhid, mixer_bufs)
        fb = [(0, 1), (2, 3), (4, 5)]
        fbi = 0
        for j in range(11):
            wa, Bwa = wget()
            wb_, Bwb = wget()
            for ch in range(2):
                k = 2 * j + ch
                for (c0, n) in sbs:
                    ba, bb = fb[fbi % 3]
                    fbi += 1
                    for kc in range(8):
                        mm(pb[ba][:, 0:n], wa[:, kc, ch * 128:ch * 128 + 128], hT[:, kc, c0:c0 + n], kc == 0, kc == 7, [Bwa] + hb(kc, c0, n), [Bpb[ba]])
                    for kc in range(8):
                        mm(pb[bb][:, 0:n], wb_[:, kc, ch * 128:ch * 128 + 128], hT[:, kc, c0:c0 + n], kc == 0, kc == 7, [Bwb] + hb(kc, c0, n), [Bpb[bb]])
                    tm, Btm = tmp()
                    act(tm[:, 0:n], pb[ba][:, 0:n], AF.Silu, [Bpb[ba]], [Btm])
                    tt("dve", hid[:, k, c0:c0 + n], tm[:, 0:n], pb[bb][:, 0:n], MUL, [Btm, Bpb[bb]], [Bhid[k]])

        if bi + 1 < len(BLOCKS):
            x_dma(bi + 1, 0)
            x_dma(bi + 1, 1)
        _stage(5)
        for m in range(8):
            w0, Bw0 = wget()
            w1, Bw1 = wget()
            for (c0, n) in sbs:
                bk = nbank()
                for kc in range(NKF):
                    w_, Bw_ = (w0, Bw0) if kc < 11 else (w1, Bw1)
                    mm(pb[bk][:, 0:n], w_[:, kc % 11, :], hid[:, kc, c0:c0 + n], kc == 0, kc == NKF - 1, [Bw_, Bhid[kc]], [Bpb[bk]])
                resid(bk, m, c0, n, 40)

        _stage(6)
        finish_stats()
        nxt = bi + 1 if bi + 1 < len(BLOCKS) else None
        if nxt is not None:
            p1a_begin(nxt)
        nnx = len(BLOCKS[nxt]) if nxt is not None else 0
        def y_mul(i):
            c0 = 128 * i
            q = i % 2
            for fc in range(8):
                stt(yT[q][:, fc, :], xT[:, fc, c0:c0 + 128], vc[:, V_FINN + fc:V_FINN + fc + 1], rsb[:, c0:c0 + 128], MUL, MUL, bx(fc, c0, 128) + [Bvc] + brs(c0, 128), ByT[q])

        y_mul(0)
        for i in range(max(nch, nnx)):
            if i < nch:
                kind, idx = chs[i]
                q = i % 2
                bks = [(4, 5), (2, 3)][q]
                for fc in range(8):
                    bk = bks[fc // 4]
                    tr(pb[bk][:, (fc % 4) * 128:(fc % 4) * 128 + 128], yT[q][:, fc, :], ident, ByT[q] + [Bcst], [Bpb[bk]])
                if i + 1 < nch:
                    y_mul(i + 1)
                ta, Ba = tmp()
                tb_, Bb = tmp()
                cp("act", ta[:], pb[bks[0]][:], [Bpb[bks[0]]], [Ba])
                cp("dve", tb_[:], pb[bks[1]][:], [Bpb[bks[1]]], [Bb])
                dst = yp_d[128 * idx:128 * idx + 128, :] if kind == "p" else ys_d
                out_dmas.append(dma("sp", dst[:, 0:512], ta[:], [Ba], (), Ba))
                out_dmas.append(dma("sp", dst[:, 512:1024], tb_[:], [Bb], (), Bb))
            if i < nnx:
                p1a_chunk(nxt, i)
                if i + 2 < nnx:
                    x_dma(nxt, i + 2)

    try:
        _stage(0)
        for bi in range(len(BLOCKS)):
            run_block(bi)
    except _Stop:
        pass
    if KSTAGE < 99:
        dbg_h = dout("dbg_h", [128, 8, TBMAX])
        dbg_x = dout("dbg_x", [128, 8, TBMAX])
        Bd1, Bd2 = P.buf("dbg1"), P.buf("dbg2")
        dma("pool", dbg_h, hT[:], [b_ for r_ in Bh for b_ in r_], (), Bd1)
        dma("sp", dbg_x, xT[:], [b_ for r_ in BxT for b_ in r_], (), Bd2)

    P.op("sp", lambda h: h.nop(), extra=all_dmas)
    P.emit(es)
    es.close()
    return nc


_NC = None


def kernel(x_prompt, x_sample, state_ret, state_gla, c_prompt, c_sample,
           w_ada, b_ada, mix_norm, w_in, w_gk_up, b_gk_up, ret_norm, gla_norm,
           w_out, ffn_norm, w_gate_up, w_down, final_norm):
    global _NC
    f = lambda a: np.ascontiguousarray(np.asarray(a, dtype=np.float32))
    x_prompt, x_sample, state_ret, state_gla = f(x_prompt), f(x_sample), f(state_ret), f(state_gla)
    c_prompt, c_sample = f(c_prompt), f(c_sample)
    cst, cossw = host_consts()
    vecs = np.concatenate([f(mix_norm).reshape(8, 128), f(ffn_norm).reshape(8, 128), f(final_norm).reshape(8, 128),
                           f(ret_norm).reshape(4, 128), f(gla_norm).reshape(4, 128), f(b_gk_up).reshape(2, 128),
                           f(b_ada).reshape(48, 128)], axis=0)
    shared = {"w_ada": f(w_ada)[0], "w_in": f(w_in)[0], "w_gk": f(w_gk_up)[0], "w_out": f(w_out)[0],
              "w_gu": f(w_gate_up)[0], "w_dn": f(w_down)[0], "vecs": np.ascontiguousarray(vecs), "cst": cst, "cossw": cossw}
    in_maps = []
    for r in range(NCORES):
        m = dict(shared)
        m["xp"] = x_prompt[r]
        m["xs"] = np.ascontiguousarray(x_sample[NSQ * r:NSQ * (r + 1)].reshape(128, D))
        m["c17"] = np.ascontiguousarray(np.concatenate([c_prompt[r:r + 1], c_sample[NSQ * r:NSQ * (r + 1)]], axis=0))
        m["sret"] = np.ascontiguousarray(state_ret[0, NSQ * r:NSQ * (r + 1)].reshape(NSQ, 512, 128))
        m["sgla"] = np.ascontiguousarray(state_gla[0, NSQ * r:NSQ * (r + 1)].reshape(NSQ, 256, 128))
        in_maps.append(m)
    if _NC is None:
        _NC = build()
    res = run_bass_kernel_spmd(_NC, in_maps, core_ids=list(range(NCORES)))
    rs = res.results
    if KSTAGE < 99:
        return rs
    y_prompt = np.stack([rs[r]["yp"] for r in range(NCORES)], axis=0).astype(np.float32)
    y_sample = np.concatenate([rs[r]["ys"].reshape(NSQ, DSEQ, D) for r in range(NCORES)], axis=0).astype(np.float32)
    srp = np.stack([rs[r]["srp"].reshape(4, 128, 128) for r in range(NCORES)], axis=0)[None].astype(np.float32)
    sgp = np.stack([rs[r]["sgp"].reshape(4, 64, 128) for r in range(NCORES)], axis=0)[None].astype(np.float32)
    srs = np.concatenate([rs[r]["srs"].reshape(NSQ, 4, 128, 128) for r in range(NCORES)], axis=0)[None].astype(np.float32)
    sgs = np.concatenate([rs[r]["sgs"].reshape(NSQ, 4, 64, 128) for r in range(NCORES)], axis=0)[None].astype(np.float32)
    return (y_prompt, y_sample, srp, sgp, srs, sgs)
```
